# Optimizing a Trainium2 kernel written in Bass

```python
import jax, jax.numpy as jnp
from jax import lax
import numpy as np

D_MODEL = 1024
BATCH = 8
SEQ = 2048
DEPTH = 2
DEC_BATCH = 128
DEC_SEQ = 1
PAST_LEN = 16384
PAGE_SIZE = 128

N_MIXERS = 2
N_LRU = (DEPTH + 1) // 2
N_CMOD = DEPTH // 2
D_RNN = D_MODEL
N_RG_BLOCKS = 4
RG_BLOCK = D_RNN // N_RG_BLOCKS
RG_C = 8.0
LRU_CONV = 4
D_CONV = D_MODEL
CMOD_CONV = 31
D_FF = 3 * D_MODEL
FFN_CONV = 3
N_MEM = 256
XA_HEADS = 4
XA_HEAD_DIM = D_MODEL // XA_HEADS
EPS = 1e-6

kernel_name = "hybrid_rglru_conformer_convffn_step"


def rms_norm(x, g):
    xf = x.astype(jnp.float32)
    y = xf * lax.rsqrt(jnp.mean(xf * xf, axis=-1, keepdims=True) + EPS) * g.astype(jnp.float32)
    return y.astype(x.dtype)


def layer_norm(x, g, b):
    xf = x.astype(jnp.float32)
    mu = jnp.mean(xf, axis=-1, keepdims=True)
    xc = xf - mu
    var = jnp.mean(xc * xc, axis=-1, keepdims=True)
    y = xc * lax.rsqrt(var + EPS) * g.astype(jnp.float32) + b.astype(jnp.float32)
    return y.astype(x.dtype)


def causal_dwconv(x, buf, w, b):
    xp = jnp.concatenate([buf.astype(x.dtype), x], axis=1)
    y = lax.conv_general_dilated(
        xp, w[:, None, :].astype(x.dtype), window_strides=(1,), padding='VALID',
        dimension_numbers=('NWC', 'WIO', 'NWC'), feature_group_count=x.shape[-1])
    return y + b.astype(x.dtype), xp[:, -(w.shape[0] - 1):]


def rg_lru(x, gate_a_pre, gate_x_pre, lam, h0):
    f32 = jnp.float32
    log_a = -RG_C * jax.nn.sigmoid(gate_a_pre.astype(f32)) * jax.nn.softplus(-lam.astype(f32))
    a = jnp.exp(log_a)
    u = jnp.sqrt(-jnp.expm1(2.0 * log_a)) * jax.nn.sigmoid(gate_x_pre.astype(f32)) * x.astype(f32)

    def step(h, au):
        a_t, u_t = au
        h = a_t * h + u_t
        return h, h

    h_last, hs = lax.scan(step, h0.astype(f32), (jnp.swapaxes(a, 0, 1), jnp.swapaxes(u, 0, 1)))
    return jnp.swapaxes(hs, 0, 1).astype(x.dtype), h_last


def recurrent_block(h, j, st_h, st_conv, p):
    bsz, t = h.shape[0], h.shape[1]
    proj = h @ p['lru_w_in'][j]
    gate_branch, rec = proj[..., :D_RNN], proj[..., D_RNN:]
    rec, new_conv = causal_dwconv(rec, st_conv, p['lru_conv_w'][j], p['lru_conv_b'][j])
    rb = rec.reshape(bsz, t, N_RG_BLOCKS, RG_BLOCK)
    ga = jnp.einsum('btnc,ncd->btnd', rb, p['lru_wa'][j]).reshape(bsz, t, D_RNN) + p['lru_ba'][j]
    gx = jnp.einsum('btnc,ncd->btnd', rb, p['lru_wx'][j]).reshape(bsz, t, D_RNN) + p['lru_bx'][j]
    y, h_last = rg_lru(rec, ga, gx, p['lru_lambda'][j], st_h)
    out = (jax.nn.gelu(gate_branch) * y) @ p['lru_w_out'][j]
    return out, h_last, new_conv


def conformer_conv(h, j, st_conv, p):
    ab = h @ p['cm_w_pw1'][j] + p['cm_b_pw1'][j]
    g = ab[..., :D_CONV] * jax.nn.sigmoid(ab[..., D_CONV:])
    c, new_buf = causal_dwconv(g, st_conv, p['cm_dw_w'][j], p['cm_dw_b'][j])
    c = jax.nn.silu(layer_norm(c, p['cm_ln_g'][j], p['cm_ln_b'][j]))
    return c @ p['cm_w_pw2'][j], new_buf


def mem_kv(mem, i, p):
    bsz = mem.shape[0]
    kv = rms_norm(mem, p['mem_norm'][i]) @ p['xa_w_kv'][i]
    k = kv[..., :D_MODEL].reshape(bsz, N_MEM, XA_HEADS, XA_HEAD_DIM)
    v = kv[..., D_MODEL:].reshape(bsz, N_MEM, XA_HEADS, XA_HEAD_DIM)
    return k, v


def cross_attn(h, k, v, i, p):
    bsz, t = h.shape[0], h.shape[1]
    q = (h @ p['xa_w_q'][i]).reshape(bsz, t, XA_HEADS, XA_HEAD_DIM)
    s = jnp.einsum('bthd,bmhd->bhtm', q.astype(jnp.float32), k.astype(jnp.float32)) * (XA_HEAD_DIM ** -0.5)
    pr = jax.nn.softmax(s, axis=-1)
    o = jnp.einsum('bhtm,bmhd->bthd', pr.astype(v.dtype), v).reshape(bsz, t, D_MODEL)
    return o.astype(h.dtype) @ p['xa_w_o'][i]


def conv_ffn(h, i, st_conv, p):
    gu = h @ p['ffn_w_up'][i]
    g, u = gu[..., :D_FF], gu[..., D_FF:]
    g, new_buf = causal_dwconv(g, st_conv, p['ffn_conv_w'][i], p['ffn_conv_b'][i])
    return (jax.nn.gelu(g) * u) @ p['ffn_w_down'][i], new_buf


def trunk(x, lru_h, lru_conv, cmod_conv, ffn_conv, mem_k, mem_v, p):
    new_h, new_lconv, new_cconv, new_fconv = [], [], [], []
    for i in range(DEPTH):
        j = i // N_MIXERS
        hn = rms_norm(x, p['norm_mix'][i])
        if i % N_MIXERS == 0:
            out, h_last, cbuf = recurrent_block(hn, j, lru_h[j], lru_conv[j], p)
            new_h.append(h_last)
            new_lconv.append(cbuf)
        else:
            out, cbuf = conformer_conv(hn, j, cmod_conv[j], p)
            new_cconv.append(cbuf)
        x = x + out
        x = x + cross_attn(rms_norm(x, p['norm_xa'][i]), mem_k[i], mem_v[i], i, p)
        out, fbuf = conv_ffn(rms_norm(x, p['norm_ffn'][i]), i, ffn_conv[i], p)
        x = x + out
        new_fconv.append(fbuf)
    y = rms_norm(x, p['norm_final'])
    return y, jnp.stack(new_h), jnp.stack(new_lconv), jnp.stack(new_cconv), jnp.stack(new_fconv)


def setup_inputs(seed: int = 0) -> dict:
    key = jax.random.key(seed)
    ks = iter(jax.random.split(key, 64))
    f32 = jnp.float32

    def nrm(shape, scale):
        return jax.random.normal(next(ks), shape, f32) * scale

    def gain(shape):
        return 1.0 + nrm(shape, 0.01)

    a0 = jax.random.uniform(next(ks), (N_LRU, D_RNN), f32, 0.9, 0.999)
    a_base = a0 ** (1.0 / RG_C)
    lru_lambda = jnp.log(a_base) - jnp.log1p(-a_base)
    return {
        'x_prompt': nrm((BATCH, SEQ, D_MODEL), 1.0),
        'x_sample': nrm((DEC_BATCH, DEC_SEQ, D_MODEL), 1.0),
        'state_lru_h': nrm((N_LRU, DEC_BATCH, D_RNN), 0.5),
        'state_lru_conv': nrm((N_LRU, DEC_BATCH, LRU_CONV - 1, D_RNN), 1.0),
        'state_cmod_conv': nrm((N_CMOD, DEC_BATCH, CMOD_CONV - 1, D_CONV), 1.0),
        'state_ffn_conv': nrm((DEPTH, DEC_BATCH, FFN_CONV - 1, D_FF), 1.0),
        'cache_mem_k': nrm((DEPTH, DEC_BATCH, N_MEM, XA_HEADS, XA_HEAD_DIM), 1.0),
        'cache_mem_v': nrm((DEPTH, DEC_BATCH, N_MEM, XA_HEADS, XA_HEAD_DIM), 1.0),
        'mem_prompt': nrm((BATCH, N_MEM, D_MODEL), 1.0),
        'norm_mix': gain((DEPTH, D_MODEL)),
        'norm_xa': gain((DEPTH, D_MODEL)),
        'norm_ffn': gain((DEPTH, D_MODEL)),
        'norm_final': gain((D_MODEL,)),
        'lru_w_in': nrm((N_LRU, D_MODEL, 2 * D_RNN), D_MODEL ** -0.5),
        'lru_conv_w': nrm((N_LRU, LRU_CONV, D_RNN), LRU_CONV ** -0.5),
        'lru_conv_b': nrm((N_LRU, D_RNN), 0.01),
        'lru_wa': nrm((N_LRU, N_RG_BLOCKS, RG_BLOCK, RG_BLOCK), RG_BLOCK ** -0.5),
        'lru_ba': nrm((N_LRU, D_RNN), 0.01),
        'lru_wx': nrm((N_LRU, N_RG_BLOCKS, RG_BLOCK, RG_BLOCK), RG_BLOCK ** -0.5),
        'lru_bx': nrm((N_LRU, D_RNN), 0.01),
        'lru_lambda': lru_lambda,
        'lru_w_out': nrm((N_LRU, D_RNN, D_MODEL), D_RNN ** -0.5),
        'cm_w_pw1': nrm((N_CMOD, D_MODEL, 2 * D_CONV), D_MODEL ** -0.5),
        'cm_b_pw1': nrm((N_CMOD, 2 * D_CONV), 0.01),
        'cm_dw_w': nrm((N_CMOD, CMOD_CONV, D_CONV), CMOD_CONV ** -0.5),
        'cm_dw_b': nrm((N_CMOD, D_CONV), 0.01),
        'cm_ln_g': gain((N_CMOD, D_CONV)),
        'cm_ln_b': nrm((N_CMOD, D_CONV), 0.01),
        'cm_w_pw2': nrm((N_CMOD, D_CONV, D_MODEL), D_CONV ** -0.5),
        'mem_norm': gain((DEPTH, D_MODEL)),
        'xa_w_q': nrm((DEPTH, D_MODEL, D_MODEL), D_MODEL ** -0.5),
        'xa_w_kv': nrm((DEPTH, D_MODEL, 2 * D_MODEL), D_MODEL ** -0.5),
        'xa_w_o': nrm((DEPTH, D_MODEL, D_MODEL), D_MODEL ** -0.5),
        'ffn_w_up': nrm((DEPTH, D_MODEL, 2 * D_FF), D_MODEL ** -0.5),
        'ffn_conv_w': nrm((DEPTH, FFN_CONV, D_FF), FFN_CONV ** -0.5),
        'ffn_conv_b': nrm((DEPTH, D_FF), 0.01),
        'ffn_w_down': nrm((DEPTH, D_FF, D_MODEL), D_FF ** -0.5),
    }


def reference(x_prompt, x_sample, state_lru_h, state_lru_conv, state_cmod_conv, state_ffn_conv,
              cache_mem_k, cache_mem_v, mem_prompt,
              norm_mix, norm_xa, norm_ffn, norm_final,
              lru_w_in, lru_conv_w, lru_conv_b, lru_wa, lru_ba, lru_wx, lru_bx, lru_lambda, lru_w_out,
              cm_w_pw1, cm_b_pw1, cm_dw_w, cm_dw_b, cm_ln_g, cm_ln_b, cm_w_pw2,
              mem_norm, xa_w_q, xa_w_kv, xa_w_o,
              ffn_w_up, ffn_conv_w, ffn_conv_b, ffn_w_down):
    p = dict(norm_mix=norm_mix, norm_xa=norm_xa, norm_ffn=norm_ffn, norm_final=norm_final,
             lru_w_in=lru_w_in, lru_conv_w=lru_conv_w, lru_conv_b=lru_conv_b, lru_wa=lru_wa,
             lru_ba=lru_ba, lru_wx=lru_wx, lru_bx=lru_bx, lru_lambda=lru_lambda, lru_w_out=lru_w_out,
             cm_w_pw1=cm_w_pw1, cm_b_pw1=cm_b_pw1, cm_dw_w=cm_dw_w, cm_dw_b=cm_dw_b,
             cm_ln_g=cm_ln_g, cm_ln_b=cm_ln_b, cm_w_pw2=cm_w_pw2,
             mem_norm=mem_norm, xa_w_q=xa_w_q, xa_w_kv=xa_w_kv, xa_w_o=xa_w_o,
             ffn_w_up=ffn_w_up, ffn_conv_w=ffn_conv_w, ffn_conv_b=ffn_conv_b, ffn_w_down=ffn_w_down)

    bsz = x_prompt.shape[0]
    dt = x_prompt.dtype
    kvs = [mem_kv(mem_prompt, i, p) for i in range(DEPTH)]
    mem_k_prompt = jnp.stack([kv[0] for kv in kvs])
    mem_v_prompt = jnp.stack([kv[1] for kv in kvs])
    y_prompt, new_lru_h_prompt, new_lru_conv_prompt, new_cmod_conv_prompt, new_ffn_conv_prompt = trunk(
        x_prompt,
        jnp.zeros((N_LRU, bsz, D_RNN), jnp.float32),
        jnp.zeros((N_LRU, bsz, LRU_CONV - 1, D_RNN), dt),
        jnp.zeros((N_CMOD, bsz, CMOD_CONV - 1, D_CONV), dt),
        jnp.zeros((DEPTH, bsz, FFN_CONV - 1, D_FF), dt),
        mem_k_prompt, mem_v_prompt, p)

    y_sample, new_lru_h_sample, new_lru_conv_sample, new_cmod_conv_sample, new_ffn_conv_sample = trunk(
        x_sample, state_lru_h, state_lru_conv, state_cmod_conv, state_ffn_conv,
        cache_mem_k, cache_mem_v, p)

    return (y_prompt, y_sample,
            new_lru_h_prompt, new_lru_conv_prompt, new_cmod_conv_prompt, new_ffn_conv_prompt,
            mem_k_prompt, mem_v_prompt,
            new_lru_h_sample, new_lru_conv_sample, new_cmod_conv_sample, new_ffn_conv_sample)
```

```python
import contextlib
import numpy as np
import concourse.bass as bass
import concourse.mybir as mybir
from concourse.bass_utils import run_bass_kernel_spmd

F32 = mybir.dt.float32
BF16 = mybir.dt.bfloat16
AF = mybir.ActivationFunctionType
ALU = mybir.AluOpType

D = 1024
T = 2048
NT = 512
NS = 16
DFF = 3072
NMEM = 256
EPS = 1e-6
SEM_MAX = 30000
NDMASEM = 12
GC1 = 0.044715 ** 0.5
GC2 = 0.7978845608028654

_PCOLS = [
    ('norm_mix', 16), ('norm_xa', 16), ('norm_ffn', 16), ('norm_final', 8), ('mem_norm', 16),
    ('lru_conv_w', 32), ('lru_conv_b', 8), ('lru_ba', 8), ('lru_bx', 8), ('lru_lambda', 8),
    ('cm_b_pw1', 16), ('cm_dw_w', 248), ('cm_dw_b', 8), ('cm_ln_g', 8), ('cm_ln_b', 8),
    ('ffn_conv_w', 144), ('ffn_conv_b', 48),
]
POFF = {}
_o = 0
for _n, _c in _PCOLS:
    POFF[_n] = _o
    _o += _c
NPAR = _o


def _cols(v):
    return np.ascontiguousarray(np.asarray(v, np.float32).reshape(-1, 128).T)


def pack_params(inp):
    out = np.zeros((128, NPAR), np.float32)

    def put(name, arr):
        a = _cols(arr)
        out[:, POFF[name]:POFF[name] + a.shape[1]] = a

    for n in ('norm_mix', 'norm_xa', 'norm_ffn', 'norm_final', 'mem_norm', 'lru_conv_w', 'lru_conv_b',
              'lru_ba', 'lru_bx', 'lru_lambda', 'cm_b_pw1', 'cm_dw_b', 'cm_ln_g', 'cm_ln_b',
              'ffn_conv_w', 'ffn_conv_b'):
        put(n, inp[n])
    w = np.asarray(inp['cm_dw_w'], np.float32)[0]
    w = w.reshape(31, 8, 128).transpose(2, 1, 0).reshape(128, 248)
    out[:, POFF['cm_dw_w']:POFF['cm_dw_w'] + 248] = w
    return out


class Buf:
    __slots__ = ('name', 'w', 'r')

    def __init__(self, name):
        self.name = name
        self.w = None
        self.r = {}


class Tracker:
    ENGS = ('pe', 'act', 'dve', 'pool', 'sp')

    def __init__(self, nc, stack):
        self.nc = nc
        self.stack = stack
        self.ops = {e: [] for e in self.ENGS}
        self.cur = {}
        self.known = {e: {} for e in self.ENGS}
        self.nsem = 0
        self.dpool = {}
        self.drr = {}

    def new_sem(self, tag):
        self.nsem += 1
        return self.stack.enter_context(self.nc.semaphore(f"{tag}{self.nsem}"))

    def _event(self, eng):
        c = self.cur.get(eng)
        if c is None or c[1] >= SEM_MAX:
            c = [self.new_sem('s' + eng), 0]
            self.cur[eng] = c
        c[1] += 1
        return (c[0], c[1])

    def _collect(self, eng, reads, writes, extra=()):
        need = {}
        kn = self.known[eng]

        def add(ev):
            if ev is None:
                return
            s, v = ev
            k = id(s)
            if kn.get(k, 0) >= v:
                return
            if k not in need or need[k][1] < v:
                need[k] = (s, v)

        for b in reads:
            add(b.w)
        for b in writes:
            add(b.w)
            for ev in b.r.values():
                add(ev)
        for ev in extra:
            add(ev)
        for k, (s, v) in need.items():
            kn[k] = v
        return list(need.values())

    def _commit(self, ev, reads, writes):
        for b in writes:
            b.w = ev
            b.r = {}
        for b in reads:
            if b in writes:
                continue
            b.r[id(ev[0])] = ev

    def op(self, eng, fn, reads=(), writes=(), r=None, w=None):
        reads = r if r is not None else reads
        writes = w if w is not None else writes
        waits = self._collect(eng, reads, writes)
        ev = self._event(eng)
        self.ops[eng].append((waits, fn, ev[0], 1))
        self._commit(ev, reads, writes)

    def dma(self, eng, fn, reads=(), writes=(), r=None, w=None):
        reads = r if r is not None else reads
        writes = w if w is not None else writes
        pool = self.dpool.setdefault(eng, [])
        i = self.drr.get(eng, 0)
        self.drr[eng] = i + 1
        if len(pool) < NDMASEM:
            pool.append([self.new_sem('d' + eng), 0, None])
        slot = pool[i % NDMASEM]
        waits = self._collect(eng, reads, writes, extra=[slot[2]] if slot[2] else [])
        slot[1] += 16
        ev = (slot[0], slot[1])
        slot[2] = ev
        self.ops[eng].append((waits, fn, slot[0], 16))
        self._commit(ev, reads, writes)

    def finish(self):
        evs = []
        for eng, pool in self.dpool.items():
            for slot in pool:
                if slot[2]:
                    evs.append(slot[2])
        for eng, c in self.cur.items():
            evs.append((c[0], c[1]))
        waits = self._collect('sp', (), (), extra=evs)
        self.ops['sp'].append((waits, None, None, 0))

    def emit(self, block):
        decs = {'pe': block.tensor, 'act': block.scalar, 'dve': block.vector, 'pool': block.gpsimd,
                'sp': block.sync}
        for e in self.ENGS:
            ops = self.ops[e]

            def body(eng, ops=ops):
                for waits, fn, sem, amt in ops:
                    for s, v in waits:
                        eng.wait_ge(s, v)
                    if fn is None:
                        continue
                    ins = fn(eng)
                    ins.then_inc(sem, amt)

            decs[e](body)


class _Stop(Exception):
    pass


def build_program(stage=None):
    nc = bass.Bass("TRN2", target_bir_lowering=False)

    def din(name, shape):
        return nc.dram_tensor(name, list(shape), F32, kind="ExternalInput").ap()

    def dout(name, shape):
        return nc.dram_tensor(name, list(shape), F32, kind="ExternalOutput").ap()

    xp = din("xp", [T, D]); xs = din("xs", [NS, D])
    st_h = din("st_h", [NS, D]); st_lc = din("st_lc", [NS * 3, D]); st_cc = din("st_cc", [NS * 30, D])
    st_fc = din("st_fc", [2, NS * 2, DFF])
    ck = din("ck", [2, NS, NMEM, D]); cv = din("cv", [2, NS, NMEM, D])
    memp = din("memp", [NMEM, D])
    params = din("params", [128, NPAR]); ident_d = din("ident", [128, 128])
    w_in = din("lru_w_in", [D, 2 * D]); w_a = din("lru_wa", [4, 256, 256]); w_x = din("lru_wx", [4, 256, 256])
    w_out = din("lru_w_out", [D, D])
    w_pw1 = din("cm_w_pw1", [D, 2 * D]); w_pw2 = din("cm_w_pw2", [D, D])
    w_q = din("xa_w_q", [2, D, D]); w_kv = din("xa_w_kv", [2, D, 2 * D]); w_o = din("xa_w_o", [2, D, D])
    w_up = din("ffn_w_up", [2, D, 2 * DFF]); w_dn = din("ffn_w_down", [2, DFF, D])

    yp = dout("yp", [T, D]); ys = dout("ys", [NS, D])
    o_lh_p = dout("o_lh_p", [1, D]); o_lc_p = dout("o_lc_p", [3, D]); o_cc_p = dout("o_cc_p", [30, D])
    o_fc_p = dout("o_fc_p", [2, 2, DFF])
    o_mk = dout("o_mk", [2, NMEM, D]); o_mv = dout("o_mv", [2, NMEM, D])
    o_lh_s = dout("o_lh_s", [NS, D]); o_lc_s = dout("o_lc_s", [NS, 3, D]); o_cc_s = dout("o_cc_s", [NS, 30, D])
    o_fc_s = dout("o_fc_s", [2, NS, 2, DFF])

    with contextlib.ExitStack() as stack:
        Tk = Tracker(nc, stack)

        def ckpt(n):
            if stage is not None and n == stage:
                raise _Stop()

        def sb(name, shape, dt):
            return stack.enter_context(nc.sbuf_tensor(name, list(shape), dt))

        XP = sb("XP", [128, 8, NT], F32); XS = sb("XS", [128, 8, NS], F32)
        HNP = sb("HNP", [128, 8, NT], BF16); HNS = sb("HNS", [128, 8, NS], BF16)
        RING = [sb(f"RING{i}", [128, 8192], BF16) for i in range(4)]
        WA = sb("WA", [128, 8, NT], BF16); WB = sb("WB", [128, 8, NT], BF16)
        FB = [sb(f"F{i}", [128, 2, 520], F32) for i in range(6)]
        RSTD = sb("RSTD", [128, NT], F32); LNV = sb("LNV", [128, NT], F32)
        WBS = sb("WBS", [128, 8, NS], BF16); WAS = sb("WAS", [128, 8, NS], BF16); WBS2 = sb("WBS2", [128, 8, NS], BF16)
        CVS = sb("CVS", [128, 8, NS], F32)
        PT = sb("PT", [128, 2, NT], BF16); PT2 = sb("PT2", [128, 2, NT], BF16)
        GG2 = sb("GG2", [128, 2, 520], F32); RC2 = sb("RC2", [128, 2, 520], F32)
        KT = sb("KT", [128, 2, 8, NMEM], BF16); VV = sb("VV", [128, 2, 2, D], BF16)
        PAR = sb("PAR", [128, NPAR], F32)
        DER = sb("DER", [128, 48], F32)
        CST = sb("CST", [128, 8], F32)
        IDF = sb("IDF", [128, 128], F32); IDB = sb("IDB", [128, 128], BF16); ONESB = sb("ONESB", [128, 128], BF16)
        DG = sb("DG", [128, 31, 128], BF16)
        KS = sb("KS", [128, 2, D], BF16); VS = sb("VS", [128, 2, D], BF16)
        RECH = sb("RECH", [128, 8, 3], F32); HST = sb("HST", [128, 8], F32)
        GHALO = sb("GHALO", [128, 8, 30], BF16); GST = sb("GST", [128, 8, 30], F32)
        FHALO = sb("FHALO", [128, 2, 24, 2], F32)
        RECS = sb("RECS", [128, 8, NS, 4], F32); H0S = sb("H0S", [128, 8, NS], F32)
        GBS = sb("GBS", [128, 8, NS, 31], BF16); GNS = sb("GNS", [128, 8, NS], F32)
        FSS = sb("FSS", [128, 24, NS, 3], F32)
        QS = sb("QS", [NS, D], BF16); SCORE = sb("SCORE", [128, 8], F32); PB = sb("PB", [128, 8], BF16)
        RSS = sb("RSS", [128, 4, NS], F32)
        SMALL = sb("SMALL", [128, 64], F32)
        PSUM = stack.enter_context(nc.psum_tensor("PSUM", [128, 8, 512], F32))

        bXP = [Buf(f"xp{c}") for c in range(8)]; bXS = [Buf(f"xs{c}") for c in range(8)]
        bHNP = Buf("hnp"); bHNS = Buf("hns")
        bRING = [Buf(f"ring{i}") for i in range(4)]
        bWA = Buf("wa"); bWB = Buf("wb"); bF = [Buf(f"f{i}") for i in range(6)]
        bWBh = [Buf("wbh0"), Buf("wbh1")]; bWBS = [Buf("wbs0"), Buf("wbs1")]
        bIOB = [Buf("iob0"), Buf("iob1")]
        bWBq = [Buf(f"wbq{i}") for i in range(4)]
        bWAS = Buf("was"); bWBS2 = Buf("wbs2"); bCVS = Buf("cvs")
        bLRU = {k: [Buf(k + '0'), Buf(k + '1')] for k in ('ta', 'tx', 'aa', 'mm', 'ta2', 'tx2', 'aa2')}
        for k in ('gg', 'rc', 'pt'):
            bLRU[k] = [[Buf(k + '00'), Buf(k + '01')], [Buf(k + '10'), Buf(k + '11')], [bWB, bWB]]
        bGBp = [Buf("gbp0"), Buf("gbp1")]; bTBp = [Buf("tbp0"), Buf("tbp1")]; bCVp = [Buf(f"cvp{i}") for i in range(8)]
        bGPp = [Buf("gpp0"), Buf("gpp1")]; bT1p = [Buf("t1p0"), Buf("t1p1")]; bT2p = [Buf("t2p0"), Buf("t2p1")]
        WAw = [bWA, bIOB[0], bIOB[1]]
        bPT2 = Buf("pt2"); bGG2 = Buf("gg2"); bRC2 = Buf("rc2")
        bRSTD = Buf("rstd"); bLNV = Buf("lnv"); bPT = Buf("pt"); bKT = Buf("kt"); bVV = Buf("vv")
        bMEMX = Buf("memx"); bMEMN = Buf("memn"); bPAR = Buf("par"); bDER = Buf("der"); bCST = Buf("cst")
        bID = Buf("id"); bSEL = Buf("sel"); bDG = Buf("dg"); bKS = Buf("ks"); bVS = Buf("vs")
        bRECH = Buf("rech"); bHST = Buf("hst"); bGHALO = Buf("ghalo"); bGST = Buf("gst"); bFHALO = Buf("fhalo")
        bRECS = Buf("recs"); bH0S = Buf("h0s"); bGBS = Buf("gbs"); bGNS = Buf("gns"); bFSS = Buf("fss")
        bQS = Buf("qs"); bSCORE = Buf("score"); bPB = Buf("pb"); bRSS = Buf("rss"); bSMALL = Buf("small")
        bPS = [Buf(f"ps{i}") for i in range(8)]
        psrr = [0]
        cur_g = [0]

        reserved = set()

        def bank():
            while True:
                i = psrr[0] % 8
                psrr[0] += 1
                if i not in reserved:
                    return i

        def act(out, in_, func, bias=None, scale=None, r=(), w=()):
            kw = {}
            if bias is not None:
                kw['bias'] = bias
            if scale is not None:
                kw['scale'] = scale
            Tk.op('act', lambda e: e.activation(out=out, in_=in_, func=func, **kw), r, w)

        def stt(out, in0, scalar, in1, op0, op1, r=(), w=(), accum=None, eng='dve'):
            Tk.op(eng, lambda e: e.scalar_tensor_tensor(out=out, in0=in0, scalar=scalar, in1=in1, op0=op0,
                                                       op1=op1, accum_out=accum), r, w)

        def ts(out, in0, s1, s2, op0, op1, r=(), w=(), eng='dve'):
            Tk.op(eng, lambda e: e.tensor_scalar(out=out, in0=in0, scalar1=s1, scalar2=s2, op0=op0, op1=op1), r, w)

        def tt(out, in0, in1, op, r=(), w=(), eng='dve'):
            Tk.op(eng, lambda e: e.tensor_tensor(out=out, in0=in0, in1=in1, op=op), r, w)

        def cp(out, in_, r=(), w=(), eng='dve'):
            if eng == 'act':
                Tk.op(eng, lambda e: e.activation(out=out, in_=in_, func=AF.Copy), r, w)
            else:
                Tk.op(eng, lambda e: e.tensor_copy(out=out, in_=in_), r, w)

        def mm(out, pairs, r=(), w=()):
            def fn(e):
                n = len(pairs)
                ins = None
                for i, (l, rh) in enumerate(pairs):
                    ins = e.matmul(out, l, rh, start=(i == 0), stop=(i == n - 1))
                return ins
            Tk.op('pe', fn, r, w)

        def tr(out, in_, r=(), w=()):
            npart = in_.shape[0]
            Tk.op('pe', lambda e: e.transpose(out, in_, IDF[0:npart, 0:npart]), tuple(r) + (bID,), w)

        def dma(eng, out, in_, r=(), w=()):
            Tk.dma(eng, lambda e: e.dma_start(out=out, in_=in_), r, w)

        def P(name, i=0, n=1):
            o = POFF[name] + i
            return PAR[:, o:o + n]

        dma('sp', PAR[:], params[:, :], w=[bPAR])
        dma('sp', IDF[:], ident_d[:, :], w=[bID])
        Tk.op('dve', lambda e: e.memset(ONESB[:], 1.0), w=[bID])
        Tk.op('dve', lambda e: e.memset(CST[:, 0:1], EPS), w=[bCST])
        Tk.op('dve', lambda e: e.memset(CST[:, 1:2], 1.0), w=[bCST])
        Tk.op('dve', lambda e: e.memset(CST[:, 2:3], 0.0), w=[bCST])
        C_EPS = CST[:, 0:1]; C_ONE = CST[:, 1:2]; C_ZERO = CST[:, 2:3]
        cp(IDB[:], IDF[:], r=[bID], w=[bID])
        Tk.op('dve', lambda e: e.memset(RECH[:], 0.0), w=[bRECH])
        Tk.op('dve', lambda e: e.memset(HST[:], 0.0), w=[bHST])
        Tk.op('dve', lambda e: e.memset(GHALO[:], 0.0), w=[bGHALO])
        Tk.op('dve', lambda e: e.memset(FHALO[:], 0.0), w=[bFHALO])
        ts(DER[:, 0:8], P('lru_ba', 0, 8), 0.5, None, ALU.mult, ALU.bypass, r=[bPAR], w=[bDER])
        ts(DER[:, 8:16], P('lru_bx', 0, 8), 0.5, None, ALU.mult, ALU.bypass, r=[bPAR], w=[bDER])
        act(SMALL[:, 0:8], P('lru_lambda', 0, 8), AF.Exp, scale=-1.0, r=[bPAR], w=[bSMALL])
        act(SMALL[:, 8:16], SMALL[:, 0:8], AF.Ln, bias=C_ONE, r=[bSMALL, bCST], w=[bSMALL])
        ts(DER[:, 16:24], SMALL[:, 8:16], -8.0, None, ALU.mult, ALU.bypass, r=[bSMALL], w=[bDER])
        ts(DER[:, 24:32], SMALL[:, 8:16], -4.0, None, ALU.mult, ALU.bypass, r=[bSMALL], w=[bDER])
        ts(DER[:, 32:40], P('cm_b_pw1', 8, 8), 0.5, None, ALU.mult, ALU.bypass, r=[bPAR], w=[bDER])
        ts(DER[:, 40:48], P('cm_ln_b', 0, 8), 0.5, None, ALU.mult, ALU.bypass, r=[bPAR], w=[bDER])

        def load_unit(i, parts):
            for off, shp, src in parts:
                n = int(np.prod(shp))
                dst = RING[i][:, off:off + n]
                if len(shp) == 2:
                    dst = dst.rearrange("p (a b) -> p a b", a=shp[0])
                elif len(shp) == 3:
                    dst = dst.rearrange("p (a b c) -> p a b c", a=shp[0], b=shp[1])
                dma('pool', dst, src, w=[bRING[i]])

        def wview(i, off, shp):
            n = int(np.prod(shp))
            v = RING[i][:, off:off + n]
            if len(shp) == 2:
                v = v.rearrange("p (a b) -> p a b", a=shp[0])
            elif len(shp) == 3:
                v = v.rearrange("p (a b c) -> p a b c", a=shp[0], b=shp[1])
            return v

        def kxn(dram2d):
            return dram2d.rearrange("(k p) n -> p k n", p=128)

        def unit_list():
            ul = []
            for li in range(2):
                ul.append((f'wk{li}', [(0, [8, 1024], kxn(w_kv[li, :, 0:1024]))]))
                ul.append((f'wv{li}', [(0, [8, 1024], kxn(w_kv[li, :, 1024:2048]))]))
            for g in range(4):
                for li in range(2):
                    if li == 0:
                        ul.append((f'{g}win_g', [(0, [8, 1024], kxn(w_in[:, 0:1024]))]))
                        ul.append((f'{g}win_r', [(0, [8, 1024], kxn(w_in[:, 1024:2048]))]))
                        ul.append((f'{g}wax', [(0, [4, 2, 256], w_a.rearrange("n (j p) d -> p n j d", p=128)),
                                               (2048, [4, 2, 256], w_x.rearrange("n (j p) d -> p n j d", p=128))]))
                        ul.append((f'{g}wout', [(0, [8, 1024], kxn(w_out[:, :]))]))
                    else:
                        ul.append((f'{g}pw1a', [(0, [8, 1024], kxn(w_pw1[:, 0:1024]))]))
                        ul.append((f'{g}pw1b', [(0, [8, 1024], kxn(w_pw1[:, 1024:2048]))]))
                        ul.append((f'{g}pw2', [(0, [8, 1024], kxn(w_pw2[:, :]))]))
                    ul.append((f'{g}wq{li}', [(0, [8, 1024], kxn(w_q[li, :, :]))]))
                    ul.append((f'{g}wo{li}', [(0, [8, 1024], kxn(w_o[li, :, :]))]))
                    for pc in range(6):
                        ul.append((f'{g}up{li}_{pc}',
                                   [(0, [8, 512], kxn(w_up[li, :, pc * 512:(pc + 1) * 512])),
                                    (4096, [8, 512], kxn(w_up[li, :, DFF + pc * 512:DFF + (pc + 1) * 512]))]))
                        ul.append((f'{g}dn{li}_{pc}', [(0, [4, 1024], kxn(w_dn[li, pc * 512:(pc + 1) * 512, :]))]))
            return ul

        class Units:
            def __init__(self, ul):
                self.ul = ul
                self.names = [n for n, _ in ul]
                self.pos = 0
                self.slot = {}
                self.free = [0, 1, 2, 3]
                self.pfx = ''

            def _fill(self):
                while self.pos < len(self.ul) and self.free:
                    name, parts = self.ul[self.pos]
                    i = self.free.pop(0)
                    load_unit(i, parts)
                    self.slot[name] = i
                    self.pos += 1

            def get(self, name):
                name = self.pfx + name if (self.pfx + name) in self.names else name
                idx = self.names.index(name)
                self._fill()
                assert idx < self.pos, (name, "ring full")
                return self.slot[name]

            def rel(self, name):
                name = self.pfx + name if (self.pfx + name) in self.names else name
                self.free.append(self.slot[name])
                self._fill()

        class TL:
            pass
        tp = TL(); tp.N = NT; tp.X = XP; tp.HN = HNP; tp.xb = bXP; tp.hb = bHNP; tp.samp = False
        tsm = TL(); tsm.N = NS; tsm.X = XS; tsm.HN = HNS; tsm.xb = bXS; tsm.hb = bHNS; tsm.samp = True

        IOB = WA[:].bitcast(F32).rearrange("p (a b) c -> p a (b c)", a=2)

        def load_T(src_rows, nrows, dstfn, wbufs, half=0, inre=None):
            dma('sp', IOB[0:nrows, half, :], src_rows, w=[bIOB[half], bWA])
            for c0 in range(0, 8, 4):
                b = bank()
                for j in range(4):
                    c = c0 + j
                    tr(PSUM[:, b, j * 128:j * 128 + nrows], IOB[0:nrows, half, c * 128:(c + 1) * 128],
                       r=[bIOB[half]], w=[bPS[b]])
                for j in range(4):
                    c = c0 + j
                    wb = wbufs[c] if isinstance(wbufs, list) else wbufs
                    src_ap = PSUM[:, b, j * 128:j * 128 + nrows]
                    if inre is not None:
                        src_ap = inre(src_ap)
                    Tk.op('act', lambda e, c=c, src_ap=src_ap: e.activation(out=dstfn(c), in_=src_ap, func=AF.Copy),
                          [bPS[b]], [wb])

        st_half = [0]

        def store_T(srcfn, ncols, dst_rows, rbufs):
            half = st_half[0] % 2
            st_half[0] += 1
            hb_ = bIOB[half]
            for c0 in range(0, 8, 4):
                b = bank()
                rb = list({id(x): x for x in ([rbufs[c0 + j] for j in range(4)] if isinstance(rbufs, list) else [rbufs])}.values())

                def fnT(e, b=b, c0=c0):
                    ins = None
                    for j in range(4):
                        ins = e.transpose(PSUM[0:ncols, b, j * 128:(j + 1) * 128], srcfn(c0 + j), IDF[:, :])
                    return ins
                Tk.op('pe', fnT, rb + [bID], [bPS[b]])
                cp(IOB[0:ncols, half, c0 * 128:(c0 + 4) * 128], PSUM[0:ncols, b, :], r=[bPS[b]], w=[hb_, bWA], eng='act')
            dma('sp', dst_rows, IOB[0:ncols, half, :], r=[hb_])

        def rmsnorm(tl, gname, gi, dst=None, dstb=None, xsrc=None, xbufs=None, N=None, out_f32=None):
            N = N or tl.N
            X = xsrc if xsrc is not None else tl.X
            xb = xbufs if xbufs is not None else tl.xb
            dst = dst if dst is not None else tl.HN
            dstb = dstb if dstb is not None else tl.hb
            b = bank()
            for q in range(4):
                act(WB[:, 2 * q:2 * q + 2, 0:N], X[:, 2 * q:2 * q + 2, 0:N], AF.Square,
                    r=list(xb[2 * q:2 * q + 2]), w=([bWBq[q], bWB] if q == 0 else [bWBq[q]]))

                def fnq(e, q=q, b=b, N=N):
                    ins = None
                    for k in (2 * q, 2 * q + 1):
                        ins = e.matmul(PSUM[:, b, 0:N], ONESB[:], WB[:, k, 0:N], start=(k == 0), stop=(k == 7))
                    return ins
                Tk.op('pe', fnq, [bWBq[q], bWB, bID], [bPS[b]])
            act(LNV[:, 0:N], PSUM[:, b, 0:N], AF.Ln, bias=C_EPS, scale=1.0 / D, r=[bPS[b], bCST], w=[bLNV])
            act(PSUM[:, b, 0:N], LNV[:, 0:N], AF.Exp, scale=-0.5, r=[bLNV], w=[bPS[b]])
            for c in range(8):
                o = out_f32(c) if out_f32 is not None else dst[:, c, 0:N]
                stt(o, X[:, c, 0:N], P(gname, gi * 8 + c), PSUM[:, b, 0:N], ALU.mult, ALU.mult,
                    r=[xb[c], bPS[b], bPAR], w=[dstb])

        def proj_residual(tl, wv, src, srcb, ringb, nk, scale=None, koff=0):
            N = tl.N
            for co in range(8):
                b = bank()
                mm(PSUM[:, b, 0:N], [(wv[:, k, co * 128:(co + 1) * 128], src[:, koff + k, 0:N]) for k in range(nk)],
                   r=[srcb, ringb], w=[bPS[b]])
                if scale is None:
                    tt(tl.X[:, co, 0:N], tl.X[:, co, 0:N], PSUM[:, b, 0:N], ALU.add, r=[bPS[b]], w=[tl.xb[co]])
                else:
                    stt(tl.X[:, co, 0:N], PSUM[:, b, 0:N], scale, tl.X[:, co, 0:N], ALU.mult, ALU.add,
                        r=[bPS[b]], w=[tl.xb[co]])

        def gelu2(out_ap, pre, t1, t2, r, wt1, wt2, wout):
            act(t1, pre, AF.Square, scale=GC1, r=r, w=[wt1])
            stt(t2, t1, 1.0, pre, ALU.add, ALU.mult, r=list(r) + [wt1], w=[wt2])
            act(t1, t2, AF.Tanh, scale=GC2, r=[wt2], w=[wt1])
            stt(out_ap, t1, 1.0, pre, ALU.add, ALU.mult, r=list(r) + [wt1], w=[wout])

        def lru_sublayer(tl, U, rel):
            N = tl.N
            rmsnorm(tl, 'norm_mix', 0)
            ig = U.get('win_g'); ir = U.get('win_r'); ia = U.get('wax'); io = U.get('wout')
            Wg = wview(ig, 0, [8, 1024]); Wr = wview(ir, 0, [8, 1024])
            Wa = wview(ia, 0, [4, 2, 256]); Wx = wview(ia, 2048, [4, 2, 256]); Wo = wview(io, 0, [8, 1024])
            WB32 = WB[:].bitcast(F32).rearrange("p (a b) c -> p a (b c)", a=4)
            GGs = [FB[0], GG2, WB32[:, 0:2, :]]; bGGs = bLRU['gg']
            RCs = [FB[1], RC2, WB32[:, 2:4, :]]; bRCs = bLRU['rc']
            PTs = [PT, PT2]; bPTs = bLRU['pt']
            MM = FB[5]
            bMMj = bLRU['mm']
            TAs = [FB[2], KS[:].bitcast(F32)]
            TXs = [FB[3], VS[:].bitcast(F32)]
            AAs = [FB[4], FSS[:].rearrange("p a b c -> p (a b c)")[:, 0:1024].rearrange("p (j n) -> p j n", j=2)]
            bTAs = [bLRU['ta'], bLRU['ta2']]; bTXs = [bLRU['tx'], bLRU['tx2']]; bAAs = [bLRU['aa'], bLRU['aa2']]

            def front(n):
                GG = GGs[n % 3]; bGGj = bGGs[n % 3]; RC = RCs[n % 3]; bRCj = bRCs[n % 3]
                PTn = PTs[n % 2]; bPTnj = bPTs[n % 2]
                bgs = []; brs = []
                for j in range(2):
                    c = 2 * n + j
                    bg = bank()
                    mm(PSUM[:, bg, 0:N], [(Wg[:, k, c * 128:(c + 1) * 128], tl.HN[:, k, 0:N]) for k in range(8)],
                       r=[tl.hb, bRING[ig]], w=[bPS[bg]])
                    br = bank()
                    mm(PSUM[:, br, 0:N], [(Wr[:, k, c * 128:(c + 1) * 128], tl.HN[:, k, 0:N]) for k in range(8)],
                       r=[tl.hb, bRING[ir]], w=[bPS[br]])
                    bgs.append(bg); brs.append(br)
                srcs = []
                for j in range(2):
                    c = 2 * n + j
                    bg = bgs[j]; br = brs[j]
                    act(GG[:, j, 0:N], PSUM[:, bg, 0:N], AF.Gelu_apprx_tanh, r=[bPS[bg]], w=[bGGj[j]])
                    if not tl.samp:
                        cp(MM[:, j, 0:3], RECH[:, c, :], r=[bRECH], w=[bMMj[j]], eng='dve')
                        cp(MM[:, j, 3:3 + N], PSUM[:, br, 0:N], r=[bPS[br]], w=[bMMj[j]], eng='act')
                        cp(RECH[:, c, :], MM[:, j, N:N + 3], r=[bMMj[j]], w=[bRECH], eng='dve')
                        srcs.append((lambda k, j=j: MM[:, j, k:k + N], bMMj[j]))
                    else:
                        cp(RECS[:, c, :, 3], PSUM[:, br, 0:N], r=[bPS[br]], w=[bRECS], eng='act')
                        srcs.append((lambda k, c=c: RECS[:, c, :, k], bRECS))
                for k in range(4):
                    for j in range(2):
                        c = 2 * n + j
                        src, srcb = srcs[j]
                        if k == 0:
                            ts(RC[:, j, 0:N], src(0), P('lru_conv_w', c), P('lru_conv_b', c), ALU.mult, ALU.add,
                               r=[srcb, bPAR], w=[bRCj[j]])
                        else:
                            stt(RC[:, j, 0:N], src(k), P('lru_conv_w', k * 8 + c), RC[:, j, 0:N], ALU.mult, ALU.add,
                                r=[srcb, bPAR], w=[bRCj[j]])
                for j in range(2):
                    cp(PTn[:, j, 0:N], RC[:, j, 0:N], r=[bRCj[j]], w=[bPTnj[j]], eng='pool')

            def back(n):
                GG = GGs[n % 3]; bGGj = bGGs[n % 3]; RC = RCs[n % 3]; bRCj = bRCs[n % 3]
                PTn = PTs[n % 2]; bPTnj = bPTs[n % 2]
                TA = TAs[n % 2]; TX = TXs[n % 2]; AA = AAs[n % 2]
                bTAj = bTAs[n % 2]; bTXj = bTXs[n % 2]; bAAj = bAAs[n % 2]
                bas = []; bxs = []
                for j in range(2):
                    ba_ = bank()
                    mm(PSUM[:, ba_, 0:N], [(Wa[:, n, jk, j * 128:(j + 1) * 128], PTn[:, jk, 0:N]) for jk in range(2)],
                       r=bPTnj + [bRING[ia]], w=[bPS[ba_]])
                    bx_ = bank()
                    mm(PSUM[:, bx_, 0:N], [(Wx[:, n, jk, j * 128:(j + 1) * 128], PTn[:, jk, 0:N]) for jk in range(2)],
                       r=bPTnj + [bRING[ia]], w=[bPS[bx_]])
                    bas.append(ba_); bxs.append(bx_)
                for j in range(2):
                    c = 2 * n + j
                    act(TA[:, j, 0:N], PSUM[:, bas[j], 0:N], AF.Tanh, bias=DER[:, c:c + 1], scale=0.5,
                        r=[bPS[bas[j]], bDER], w=[bTAj[j]])
                    act(TX[:, j, 0:N], PSUM[:, bxs[j], 0:N], AF.Tanh, bias=DER[:, 8 + c:9 + c], scale=0.5,
                        r=[bPS[bxs[j]], bDER], w=[bTXj[j]])
                for j in range(2):
                    c = 2 * n + j
                    act(AA[:, j, 0:N], TA[:, j, 0:N], AF.Exp, bias=DER[:, 24 + c:25 + c], scale=DER[:, 24 + c:25 + c],
                        r=[bTAj[j], bDER], w=[bAAj[j]])
                    act(TA[:, j, 0:N], TA[:, j, 0:N], AF.Exp, bias=DER[:, 16 + c:17 + c], scale=DER[:, 16 + c:17 + c],
                        r=[bDER], w=[bTAj[j]])
                for j in range(2):
                    act(TA[:, j, 0:N], TA[:, j, 0:N], AF.Ln, bias=C_ONE, scale=-1.0, r=[bCST], w=[bTAj[j]])
                for j in range(2):
                    act(TA[:, j, 0:N], TA[:, j, 0:N], AF.Exp, scale=0.5, w=[bTAj[j]])

            def back_b(n):
                GG = GGs[n % 3]; bGGj = bGGs[n % 3]; RC = RCs[n % 3]; bRCj = bRCs[n % 3]
                TA = TAs[n % 2]; TX = TXs[n % 2]; AA = AAs[n % 2]
                bTAj = bTAs[n % 2]; bTXj = bTXs[n % 2]; bAAj = bAAs[n % 2]
                for j in range(2):
                    stt(TX[:, j, 0:N], TX[:, j, 0:N], 1.0, TA[:, j, 0:N], ALU.add, ALU.mult, r=[bTAj[j]], w=[bTXj[j]])
                for j in range(2):
                    stt(TX[:, j, 0:N], TX[:, j, 0:N], 0.5, RC[:, j, 0:N], ALU.mult, ALU.mult, r=[bRCj[j]], w=[bTXj[j]])
                for j in range(2):
                    c = 2 * n + j
                    if not tl.samp:
                        Tk.op('dve', lambda e, j=j, c=c: e.tensor_tensor_scan(
                            out=TA[:, j, 0:N], data0=AA[:, j, 0:N], data1=TX[:, j, 0:N], initial=HST[:, c:c + 1],
                            op0=ALU.mult, op1=ALU.add), [bAAj[j], bTXj[j], bHST], [bTAj[j]])
                        cp(HST[:, c:c + 1], TA[:, j, N - 1:N], r=[bTAj[j]], w=[bHST], eng='dve')
                    else:
                        tt(TA[:, j, 0:N], AA[:, j, 0:N], H0S[:, c, :], ALU.mult, r=[bAAj[j], bH0S], w=[bTAj[j]])
                        tt(TA[:, j, 0:N], TA[:, j, 0:N], TX[:, j, 0:N], ALU.add, r=[bTXj[j]], w=[bTAj[j]])
                        cp(H0S[:, c, :], TA[:, j, 0:N], r=[bTAj[j]], w=[bH0S], eng='act')
                for j in range(2):
                    c = 2 * n + j
                    tt(WA[:, c, 0:N], TA[:, j, 0:N], GG[:, j, 0:N], ALU.mult, r=[bTAj[j], bGGj[j]], w=WAw)

            for it in range(5):
                if it < 4:
                    front(it)
                if it >= 1:
                    back(it - 1)
                    back_b(it - 1)
            if rel:
                U.rel('win_g'); U.rel('win_r'); U.rel('wax')
            proj_residual(tl, Wo, WA, bWA, bRING[io], 8)
            if rel:
                U.rel('wout')

        def cmod_sublayer(tiles, U):
            for tl in tiles:
                rmsnorm(tl, 'norm_mix', 1)
            ia = U.get('pw1a'); ib = U.get('pw1b'); i2 = U.get('pw2')
            Wa_ = wview(ia, 0, [8, 1024]); Wb_ = wview(ib, 0, [8, 1024]); W2 = wview(i2, 0, [8, 1024])
            GBUF = FB[0][:].bitcast(BF16)
            TB = FB[1]
            CVp = [FB[2], FB[3], FB[4], FB[5]]
            st = {}

            def cv_of(tl, c):
                if tl.samp:
                    return CVS[:, c, :], bCVS
                return CVp[c // 2][:, c % 2, 0:tl.N], bCVp[c]

            def lnbufs(tl):
                if tl.samp:
                    return WAS, [bWAS], bWAS, WBS2, bWBS2
                return WA, WAw, bWA, WB, bWB

            def build_dg(c):
                o = POFF['cm_dw_w'] + c * 31
                if cur_g[0] >= 1 and c % 2 == 1:
                    DGc = GBS[:].rearrange("p a b c -> p (a b c)").rearrange("p (k q) -> p k q", k=31)
                    bDGc = bGBS
                else:
                    DGc = DG[:]
                    bDGc = bDG
                Tk.op('pool', lambda e, o=o, DGc=DGc: e.tensor_tensor(
                    out=DGc, in0=IDF[:].unsqueeze(1).broadcast_to([128, 31, 128]),
                    in1=PAR[:, o:o + 31].unsqueeze(2).broadcast_to([128, 31, 128]), op=ALU.mult), [bID, bPAR], [bDGc])
                return DGc, bDGc

            def stage_a(tl, c, dg):
                N = tl.N
                q = c % 2
                bGB = bGBp[q]; bTB = bTBp[q]
                DGc, bDGc = dg
                pa = bank()
                mm(PSUM[:, pa, 0:N], [(Wa_[:, k, c * 128:(c + 1) * 128], tl.HN[:, k, 0:N]) for k in range(8)],
                   r=[tl.hb, bRING[ia]], w=[bPS[pa]])
                pb = bank()
                mm(PSUM[:, pb, 0:N], [(Wb_[:, k, c * 128:(c + 1) * 128], tl.HN[:, k, 0:N]) for k in range(8)],
                   r=[tl.hb, bRING[ib]], w=[bPS[pb]])
                act(TB[:, q, 0:N], PSUM[:, pb, 0:N], AF.Sigmoid, bias=P('cm_b_pw1', 8 + c),
                    r=[bPS[pb], bPAR], w=[bTB])
                if not tl.samp:
                    cp(GBUF[:, q, 0:30], GHALO[:, c, :], r=[bGHALO], w=[bGB], eng='act')
                    stt(GBUF[:, q, 30:30 + N], PSUM[:, pa, 0:N], P('cm_b_pw1', c), TB[:, q, 0:N], ALU.add, ALU.mult,
                        r=[bTB, bPS[pa], bPAR], w=[bGB])
                    cp(GHALO[:, c, :], GBUF[:, q, N:N + 30], r=[bGB], w=[bGHALO], eng='act')
                    stt(GST[:, c, :], PSUM[:, pa, N - 30:N], P('cm_b_pw1', c), TB[:, q, N - 30:N], ALU.add, ALU.mult,
                        r=[bTB, bPS[pa], bPAR], w=[bGST])
                    src = lambda k, q=q, N=N: GBUF[:, q, k:k + N]
                    srcb = bGB
                else:
                    stt(GBS[:, c, :, 30], PSUM[:, pa, 0:N], P('cm_b_pw1', c), TB[:, q, 0:N], ALU.add, ALU.mult,
                        r=[bTB, bPS[pa], bPAR], w=[bGBS])
                    stt(GNS[:, c, :], PSUM[:, pa, 0:N], P('cm_b_pw1', c), TB[:, q, 0:N], ALU.add, ALU.mult,
                        r=[bTB, bPS[pa], bPAR], w=[bGNS])
                    src = lambda k, c=c: GBS[:, c, :, k]
                    srcb = bGBS
                st[(id(tl), c)] = (DGc, bDGc, src, srcb)

            def stage_b(tl, c):
                N = tl.N
                DGc, bDGc, src, srcb = st[(id(tl), c)]
                WAx, WAxw, bWAx, WBx, bWBx = lnbufs(tl)
                pc_ = bank()
                mm(PSUM[:, pc_, 0:N], [(DGc[:, k, :], src(k)) for k in range(31)], r=[bDGc, srcb], w=[bPS[pc_]])
                cvc, bcv = cv_of(tl, c)
                act(cvc, PSUM[:, pc_, 0:N], AF.Identity, bias=P('cm_dw_b', c), r=[bPS[pc_], bPAR], w=[bcv])
                cp(WAx[:, c, 0:N], cvc, r=[bcv], w=WAxw, eng='act')
                act(WBx[:, c, 0:N], cvc, AF.Square, r=[bcv], w=[bWBx])

            if cur_g[0] == 0:
                for c in range(8):
                    dg = build_dg(c)
                    for tl in tiles:
                        stage_a(tl, c, dg)
                    for tl in tiles:
                        stage_b(tl, c)
            else:
                for it in range(9):
                    if it < 8:
                        dg = build_dg(it)
                        for tl in tiles:
                            stage_a(tl, it, dg)
                    if it >= 1:
                        for tl in tiles:
                            stage_b(tl, it - 1)
            U.rel('pw1a'); U.rel('pw1b')
            Z = FB[0]
            for tl in tiles:
                N = tl.N
                WAx, WAxw, bWAx, WBx, bWBx = lnbufs(tl)
                p1 = bank()
                mm(PSUM[:, p1, 0:N], [(ONESB[:], WAx[:, k, 0:N]) for k in range(8)], r=[bWAx, bID], w=[bPS[p1]])
                p2 = bank()
                mm(PSUM[:, p2, 0:N], [(ONESB[:], WBx[:, k, 0:N]) for k in range(8)], r=[bWBx, bID], w=[bPS[p2]])
                Tk.op('act', lambda e, p1=p1, N=N: e.activation(out=LNV[:, 0:N], in_=PSUM[:, p1, 0:N], func=AF.Copy,
                                                               scale=1.0 / D), [bPS[p1]], [bLNV])
                act(RSTD[:, 0:N], LNV[:, 0:N], AF.Square, r=[bLNV], w=[bRSTD])
                stt(RSTD[:, 0:N], PSUM[:, p2, 0:N], 1.0 / D, RSTD[:, 0:N], ALU.mult, ALU.subtract, r=[bPS[p2]], w=[bRSTD])
                act(RSTD[:, 0:N], RSTD[:, 0:N], AF.Ln, bias=C_EPS, r=[bCST], w=[bRSTD])
                act(RSTD[:, 0:N], RSTD[:, 0:N], AF.Exp, scale=-0.5, w=[bRSTD])
                for c in range(8):
                    q = c % 2
                    bZ = bGBp[q]
                    cvc, bcv = cv_of(tl, c)
                    tt(Z[:, q, 0:N], cvc, LNV[:, 0:N], ALU.subtract, r=[bcv, bLNV], w=[bZ])
                    stt(Z[:, q, 0:N], Z[:, q, 0:N], P('cm_ln_g', c), RSTD[:, 0:N], ALU.mult, ALU.mult, r=[bRSTD, bPAR], w=[bZ])
                    act(WAx[:, c, 0:N], Z[:, q, 0:N], AF.Silu, bias=P('cm_ln_b', c), r=[bZ, bPAR], w=WAxw)
                proj_residual(tl, W2, WAx, bWAx, bRING[i2], 8)
            U.rel('pw2')

        def mem_kv(li, U):
            ik = U.get(f'wk{li}'); iv = U.get(f'wv{li}')
            Wk = wview(ik, 0, [8, 1024]); Wv = wview(iv, 0, [8, 1024])
            class M: pass
            m = M(); m.N = NMEM
            MEMN = HNP
            bMEMN = bHNP
            rmsnorm(m, 'mem_norm', li, dst=HNP, dstb=bHNP, xsrc=XP, xbufs=bXP, N=NMEM)
            for c in range(8):
                b = bank()
                mm(PSUM[:, b, 0:NMEM], [(Wk[:, k, c * 128:(c + 1) * 128], MEMN[:, k, 0:NMEM]) for k in range(8)],
                   r=[bMEMN, bRING[ik]], w=[bPS[b]])
                cp(KT[:, li, c, :], PSUM[:, b, 0:NMEM], r=[bPS[b]], w=[bKT], eng='act')
            for (Wm, ib_, odram, isv) in ((Wk, ik, o_mk, False), (Wv, iv, o_mv, True)):
                for mc in range(2):
                    ih = 0
                    for hf in range(2):
                        b = bank()
                        mm(PSUM[:, b, :], [(MEMN[:, k, mc * 128:(mc + 1) * 128], Wm[:, k, hf * 512:(hf + 1) * 512])
                                           for k in range(8)], r=[bMEMN, bRING[ib_]], w=[bPS[b]])
                        cp(IOB[:, ih, hf * 512:(hf + 1) * 512], PSUM[:, b, :], r=[bPS[b]], w=[bIOB[ih], bWA], eng='act')
                        if isv:
                            cp(VV[:, li, mc, hf * 512:(hf + 1) * 512], PSUM[:, b, :], r=[bPS[b]], w=[bVV], eng='dve')
                    dma('sp', odram[li, mc * 128:(mc + 1) * 128, :], IOB[:, ih, :], r=[bIOB[ih]])
            U.rel(f'wk{li}'); U.rel(f'wv{li}')

        def xattn_prompt(tl, li, U, rel):
            N = tl.N
            iq = U.get(f'wq{li}'); io = U.get(f'wo{li}')
            Wq = wview(iq, 0, [8, 1024]); Wo = wview(io, 0, [8, 1024])
            rmsnorm(tl, 'norm_xa', li)
            for c in range(8):
                b = bank()
                mm(PSUM[:, b, 0:N], [(Wq[:, k, c * 128:(c + 1) * 128], tl.HN[:, k, 0:N]) for k in range(8)],
                   r=[tl.hb, bRING[iq]], w=[bPS[b]])
                cp(WB[:, c, 0:N], PSUM[:, b, 0:N], r=[bPS[b]], w=[bWB], eng='act')
            PTs = [PT, PT2]; bPTs = [bPT, bPT2]

            def sphase(h):
                PTh = PTs[h % 2]; bPTh = bPTs[h % 2]
                for mc in range(2):
                    b = bank()
                    mm(PSUM[:, b, 0:N], [(KT[:, li, 2 * h + j, mc * 128:(mc + 1) * 128], WB[:, 2 * h + j, 0:N]) for j in range(2)],
                       r=[bKT, bWB], w=[bPS[b]])
                    act(PTh[:, mc, 0:N], PSUM[:, b, 0:N], AF.Exp, scale=1.0 / 16.0, r=[bPS[b]], w=[bPTh])

            def vphase(h):
                PTh = PTs[h % 2]; bPTh = bPTs[h % 2]
                bs = bank()
                mm(PSUM[:, bs, 0:N], [(ONESB[:], PTh[:, mc, 0:N]) for mc in range(2)], r=[bPTh, bID], w=[bPS[bs]])
                act(LNV[:, 0:N], PSUM[:, bs, 0:N], AF.Ln, r=[bPS[bs]], w=[bLNV])
                act(RSTD[:, 0:N], LNV[:, 0:N], AF.Exp, scale=-1.0, r=[bLNV], w=[bRSTD])
                for j in range(2):
                    c = 2 * h + j
                    b = bank()
                    mm(PSUM[:, b, 0:N], [(VV[:, li, mc, c * 128:(c + 1) * 128], PTh[:, mc, 0:N]) for mc in range(2)],
                       r=[bVV, bPTh], w=[bPS[b]])
                    tt(WA[:, c, 0:N], PSUM[:, b, 0:N], RSTD[:, 0:N], ALU.mult, r=[bPS[b], bRSTD], w=WAw)

            for it in range(5):
                if it < 4:
                    sphase(it)
                if it >= 1:
                    vphase(it - 1)
            proj_residual(tl, Wo, WA, bWA, bRING[io], 8)

        def xattn_sample(tl, li, U, rel):
            N = NS
            iq = U.get(f'wq{li}'); io = U.get(f'wo{li}')
            Wq = wview(iq, 0, [8, 1024]); Wo = wview(io, 0, [8, 1024])
            rmsnorm(tl, 'norm_xa', li)
            for hf in range(2):
                b = bank()
                mm(PSUM[0:NS, b, :], [(tl.HN[:, k, :], Wq[:, k, hf * 512:(hf + 1) * 512]) for k in range(8)],
                   r=[tl.hb, bRING[iq]], w=[bPS[b]])
                cp(QS[:, hf * 512:(hf + 1) * 512], PSUM[0:NS, b, :], r=[bPS[b]], w=[bQS], eng='act')
            bo = bank()
            reserved.add(bo)

            def v2(t):
                return t[:].bitcast(BF16)[:, :, 0:1024] if False else t[:].bitcast(BF16).rearrange("p a b -> p (a b)")[:, 0:2048].rearrange("p (a b) -> p a b", a=2)
            slots = [
                (KS[:], VS[:], bKS, bVS),
                (HNP[:, 0:4, :].rearrange("p (a b) c -> p a (b c)", a=2), HNP[:, 4:8, :].rearrange("p (a b) c -> p a (b c)", a=2), bHNP, bHNP),
                (WB[:, 0:4, :].rearrange("p (a b) c -> p a (b c)", a=2), WB[:, 4:8, :].rearrange("p (a b) c -> p a (b c)", a=2), bWB, bWB),
                (v2(FB[1]), v2(FB[2]), bF[1], bF[2]),
                (v2(FB[3]), v2(FB[4]), bF[3], bF[4]),
            ]
            NSL = len(slots)

            def kv_load(s2):
                kd, vd, kb, vb = slots[s2 % NSL]
                dma('pool', kd, ck[li, s2].rearrange("(mc p) d -> p mc d", p=128), r=tl.xb, w=[kb])
                dma('pool', vd, cv[li, s2].rearrange("(mc p) d -> p mc d", p=128), r=tl.xb, w=[vb])

            for s2 in range(min(NSL - 1, NS)):
                kv_load(s2)
            def qbc(s):
                qb = []
                for hf in range(2):
                    b = bank()
                    mm(PSUM[:, b, :], [(IDB[0:NS, s:s + 1].broadcast_to([NS, 128]), QS[:, hf * 512:(hf + 1) * 512])],
                       r=[bID, bQS], w=[bPS[b]])
                    qb.append(b)
                return qb

            qb_next = qbc(0)
            for s in range(NS):
                KSs, VSs, bKSs, bVSs = slots[s % NSL]
                if s + NSL - 1 < NS:
                    kv_load(s + NSL - 1)
                qb = qb_next
                if s + 1 < NS:
                    qb_next = qbc(s + 1)
                for mc in range(2):
                    for h in range(4):
                        b = qb[h // 2]
                        stt(FB[0][:, 0, 0:256], KSs[:, mc, h * 256:(h + 1) * 256], 1.0,
                            PSUM[:, b, (h % 2) * 256:(h % 2 + 1) * 256], ALU.mult, ALU.mult,
                            r=[bKSs, bPS[b]], w=[bF[0], bSCORE], accum=SCORE[:, mc * 4 + h:mc * 4 + h + 1])
                act(PB[:], SCORE[:], AF.Exp, scale=1.0 / 16.0, r=[bSCORE], w=[bPB])

                def fn(e, s=s, VSs=VSs):
                    ins = None
                    for c in range(8):
                        h = c // 2
                        for mc in range(2):
                            ins = e.matmul(PSUM[:, bo, c * NS + s:c * NS + s + 1], VSs[:, mc, c * 128:(c + 1) * 128],
                                           PB[:, mc * 4 + h:mc * 4 + h + 1], start=(mc == 0), stop=(mc == 1))
                    for h in range(4):
                        for mc in range(2):
                            ins = e.matmul(PSUM[:, bo, 128 + h * NS + s:128 + h * NS + s + 1], ONESB[:],
                                           PB[:, mc * 4 + h:mc * 4 + h + 1], start=(mc == 0), stop=(mc == 1))
                    return ins
                Tk.op('pe', fn, [bVSs, bPB, bID], [bPS[bo]])
            Tk.op('dve', lambda e: e.reciprocal(out=RSS[:].rearrange("p h s -> p (h s)"), in_=PSUM[:, bo, 128:192]),
                  [bPS[bo]], [bRSS])
            for c in range(8):
                tt(WA[:, c, 0:NS], PSUM[:, bo, c * NS:(c + 1) * NS], RSS[:, c // 2, :], ALU.mult,
                   r=[bPS[bo], bRSS], w=WAw)
            reserved.discard(bo)
            proj_residual(tl, Wo, WA, bWA, bRING[io], 8)

        def ffn_sublayer(tiles, li, U):
            for tl in tiles:
                rmsnorm(tl, 'norm_ffn', li)
            GP = FB[0]; bGP = bF[0]
            T1 = FB[1]; T2 = FB[2]; bT1 = bF[1]; bT2 = bF[2]; T3 = FB[3]; bT3 = bF[3]

            def hgu(tl, par):
                if tl.samp:
                    return WBS, 4 * par, bWBS[par]
                return WB, 4 * par, bWBh[par]

            def up(pc):
                iu = U.get(f'up{li}_{pc}')
                Wg = wview(iu, 0, [8, 512]); Wu = wview(iu, 4096, [8, 512])
                for tl in tiles:
                    N = tl.N
                    HG, ko, bHG = hgu(tl, pc % 2)
                    for fl in range(4):
                        f = pc * 4 + fl
                        par = fl % 2
                        bg = bank()
                        bu = bank()

                        def fn2(e, bg=bg, bu=bu, fl=fl, tl=tl, N=N):
                            ins = None
                            for k in range(8):
                                ins = e.matmul(PSUM[:, bg, 0:N], Wg[:, k, fl * 128:(fl + 1) * 128], tl.HN[:, k, 0:N],
                                               start=(k == 0), stop=(k == 7))
                            for k in range(8):
                                ins = e.matmul(PSUM[:, bu, 0:N], Wu[:, k, fl * 128:(fl + 1) * 128], tl.HN[:, k, 0:N],
                                               start=(k == 0), stop=(k == 7))
                            return ins
                        Tk.op('pe', fn2, [tl.hb, bRING[iu]], [bPS[bg], bPS[bu]])
                        if not tl.samp:
                            cp(GP[:, par, 0:2], FHALO[:, li, f, :], r=[bFHALO], w=[bGPp[par]], eng='act')
                            cp(GP[:, par, 2:2 + N], PSUM[:, bg, 0:N], r=[bPS[bg]], w=[bGPp[par]], eng='act')
                            cp(FHALO[:, li, f, :], GP[:, par, N:N + 2], r=[bGPp[par]], w=[bFHALO], eng='act')
                            src = lambda k, par=par, N=N: GP[:, par, k:k + N]
                            srcb = bGPp[par]
                        else:
                            cp(FSS[:, f, :, 2], PSUM[:, bg, 0:N], r=[bPS[bg]], w=[bFSS], eng='act')
                            src = lambda k, f=f: FSS[:, f, :, k]
                            srcb = bFSS
                        cw = lambda k, f=f: P('ffn_conv_w', li * 72 + k * 24 + f)
                        act(T1[:, par, 0:N], PSUM[:, bg, 0:N], AF.Identity, bias=P('ffn_conv_b', li * 24 + f), scale=cw(2),
                            r=[bPS[bg], bPAR], w=[bT1p[par]])
                        stt(PSUM[:, bg, 0:N], src(1), cw(1), T1[:, par, 0:N], ALU.mult, ALU.add,
                            r=[srcb, bPAR, bT1p[par]], w=[bPS[bg]])
                        stt(PSUM[:, bg, 0:N], src(0), cw(0), PSUM[:, bg, 0:N], ALU.mult, ALU.add, r=[srcb, bPAR], w=[bPS[bg]])
                        act(T2[:, par, 0:N], PSUM[:, bg, 0:N], AF.Gelu_apprx_tanh, r=[bPS[bg]], w=[bT2p[par]])
                        tt(HG[:, ko + fl, 0:N], T2[:, par, 0:N], PSUM[:, bu, 0:N], ALU.mult, r=[bT2p[par], bPS[bu]], w=[bHG])
                U.rel(f'up{li}_{pc}')

            def dn(pc):
                idn = U.get(f'dn{li}_{pc}')
                Wd = wview(idn, 0, [4, 1024])
                for tl in tiles:
                    HG, ko, bHG = hgu(tl, pc % 2)
                    proj_residual(tl, Wd, HG, bHG, bRING[idn], 4, koff=ko)
                U.rel(f'dn{li}_{pc}')

            for it in range(7):
                if it < 6:
                    up(it)
                if it >= 1:
                    dn(it - 1)

        def final_out(tl, t0):
            N = tl.N
            stg = lambda c: FB[2 + c // 2][:, c % 2, 0:N]
            b = bank()
            for q in range(4):
                act(WB[:, 2 * q:2 * q + 2, 0:N], tl.X[:, 2 * q:2 * q + 2, 0:N], AF.Square,
                    r=list(tl.xb[2 * q:2 * q + 2]), w=([bWBq[q], bWB] if q == 0 else [bWBq[q]]))

                def fnq(e, q=q, b=b, N=N):
                    ins = None
                    for k in (2 * q, 2 * q + 1):
                        ins = e.matmul(PSUM[:, b, 0:N], ONESB[:], WB[:, k, 0:N], start=(k == 0), stop=(k == 7))
                    return ins
                Tk.op('pe', fnq, [bWBq[q], bWB, bID], [bPS[b]])
            act(LNV[:, 0:N], PSUM[:, b, 0:N], AF.Ln, bias=C_EPS, scale=1.0 / D, r=[bPS[b], bCST], w=[bLNV])
            act(RSTD[:, 0:N], LNV[:, 0:N], AF.Exp, scale=-0.5, r=[bLNV], w=[bRSTD])
            for c in range(8):
                stt(stg(c), tl.X[:, c, 0:N], P('norm_final', c), RSTD[:, 0:N], ALU.mult, ALU.mult,
                    r=[tl.xb[c], bRSTD, bPAR], w=[bF[2 + c // 2]])
            fbufs = [bF[2 + c // 2] for c in range(8)]
            if tl.samp:
                store_T(lambda c: stg(c), NS, ys[:, :], fbufs)
            else:
                for tq in range(4):
                    store_T(lambda c, tq=tq: FB[2 + c // 2][:, c % 2, tq * 128:(tq + 1) * 128], 128,
                            yp[t0 + tq * 128:t0 + (tq + 1) * 128, :], fbufs)

        U = Units(unit_list())

        def main_schedule():
            for mh in range(2):
                load_T(memp[mh * 128:(mh + 1) * 128, :], 128, lambda c, mh=mh: XP[:, c, mh * 128:(mh + 1) * 128], bXP,
                       half=mh)
            ckpt(1)
            for li in range(2):
                mem_kv(li, U)
            ckpt(2)
            load_T(xs[:, :], NS, lambda c: XS[:, c, :], bXS)
            load_T(st_h[:, :], NS, lambda c: H0S[:, c, :], bH0S)
            load_T(st_lc[:, :], NS * 3, lambda c: RECS[:, c, :, 0:3], bRECS,
                   inre=lambda a: a.rearrange("p (b j) -> p b j", j=3))
            for q in range(4):
                load_T(st_cc[q * 120:(q + 1) * 120, :], 120, lambda c, q=q: GBS[:, c, q * 4:(q + 1) * 4, 0:30], bGBS,
                       inre=lambda a: a.rearrange("p (b j) -> p b j", j=30))
            ckpt(3)
            dma('sp', o_lc_s[:, 0:2, :], st_lc.rearrange("(b j) d -> b j d", j=3)[:, 1:3, :])
            dma('sp', o_cc_s[:, 0:29, :], st_cc.rearrange("(b j) d -> b j d", j=30)[:, 1:30, :])
            for li in range(2):
                dma('sp', o_fc_s[li, :, 0:1, :], st_fc[li].rearrange("(b j) d -> b j d", j=2)[:, 1:2, :])
            ckpt(4)

            for g in range(4):
                t0 = g * NT
                U.pfx = str(g)
                cur_g[0] = g
                tiles = [tp] + ([tsm] if g == 0 else [])
                nt = len(tiles)
                for hq in range(4):
                    load_T(xp[t0 + hq * 128:t0 + (hq + 1) * 128, :], 128,
                           lambda c, hq=hq: XP[:, c, hq * 128:(hq + 1) * 128], bXP, half=hq % 2)
                for li in range(2):
                    if li == 0:
                        for ti, tl in enumerate(tiles):
                            lru_sublayer(tl, U, ti == nt - 1)
                            ckpt(10 + g * 100 + li * 10 + ti)
                    else:
                        cmod_sublayer(tiles, U)
                        ckpt(10 + g * 100 + li * 10)
                    for ti, tl in enumerate(tiles):
                        if tl.samp:
                            xattn_sample(tl, li, U, True)
                        else:
                            xattn_prompt(tl, li, U, True)
                        ckpt(12 + g * 100 + li * 10 + ti)
                    U.rel(f'wq{li}'); U.rel(f'wo{li}')
                    if g == 0:
                        for f3 in range(3):
                            load_T(st_fc[li, :, f3 * 1024:(f3 + 1) * 1024], NS * 2,
                                   lambda c, f3=f3: FSS[:, f3 * 8 + c, :, 0:2], bFSS,
                                   inre=lambda a: a.rearrange("p (b j) -> p b j", j=2))
                    ffn_sublayer(tiles, li, U)
                    ckpt(14 + g * 100 + li * 10)
                    if g == 0:
                        for f3 in range(3):
                            store_T(lambda c, f3=f3: FSS[:, f3 * 8 + c, :, 2], NS,
                                    o_fc_s[li, :, 1, f3 * 1024:(f3 + 1) * 1024], bFSS)
                for ti, tl in enumerate(tiles):
                    final_out(tl, t0)
                    ckpt(90 + g * 100 + ti)
                if g == 0:
                    store_T(lambda c: H0S[:, c, :], NS, o_lh_s[:, :], bH0S)
                    ckpt(93)
                    store_T(lambda c: RECS[:, c, :, 3], NS, o_lc_s[:, 2, :], bRECS)
                    ckpt(94)
                    store_T(lambda c: GNS[:, c, :], NS, o_cc_s[:, 29, :], bGNS)
                ckpt(99 + g * 100)
            store_T(lambda c: HST[:, c:c + 1], 1, o_lh_p[:, :], bHST)
            store_T(lambda c: RECH[:, c, :], 3, o_lc_p[:, :], bRECH)
            store_T(lambda c: GST[:, c, :], 30, o_cc_p[:, :], bGST)
            for li in range(2):
                for f3 in range(3):
                    store_T(lambda c, li=li, f3=f3: FHALO[:, li, f3 * 8 + c, :], 2,
                            o_fc_p[li, :, f3 * 1024:(f3 + 1) * 1024], bFHALO)

        try:
            main_schedule()
        except _Stop:
            pass

        Tk.finish()
        with nc.Block() as block:
            Tk.emit(block)
    return nc


_NC_CACHE = {}


def kernel(**inputs):
    f = lambda k: np.asarray(inputs[k], np.float32)
    if 'nc' not in _NC_CACHE:
        _NC_CACHE['nc'] = build_program()
    nc = _NC_CACHE['nc']
    par = pack_params(inputs)
    ident = np.eye(128, dtype=np.float32)
    C = np.ascontiguousarray
    shared = {
        'params': par, 'ident': ident,
        'lru_w_in': C(f('lru_w_in')[0]), 'lru_wa': C(f('lru_wa')[0]), 'lru_wx': C(f('lru_wx')[0]),
        'lru_w_out': C(f('lru_w_out')[0]), 'cm_w_pw1': C(f('cm_w_pw1')[0]), 'cm_w_pw2': C(f('cm_w_pw2')[0]),
        'xa_w_q': C(f('xa_w_q')), 'xa_w_kv': C(f('xa_w_kv')), 'xa_w_o': C(f('xa_w_o')),
        'ffn_w_up': C(f('ffn_w_up')), 'ffn_w_down': C(f('ffn_w_down')),
    }
    in_maps = []
    for b in range(8):
        sl = slice(NS * b, NS * (b + 1))
        m = dict(shared)
        m.update({
            'xp': C(f('x_prompt')[b]), 'xs': C(f('x_sample')[sl, 0, :]),
            'st_h': C(f('state_lru_h')[0, sl]),
            'st_lc': C(f('state_lru_conv')[0, sl].reshape(NS * 3, D)),
            'st_cc': C(f('state_cmod_conv')[0, sl].reshape(NS * 30, D)),
            'st_fc': C(f('state_ffn_conv')[:, sl].reshape(2, NS * 2, DFF)),
            'ck': C(f('cache_mem_k')[:, sl].reshape(2, NS, NMEM, D)),
            'cv': C(f('cache_mem_v')[:, sl].reshape(2, NS, NMEM, D)),
            'memp': C(f('mem_prompt')[b]),
        })
        in_maps.append(m)
    res = run_bass_kernel_spmd(nc, in_maps, core_ids=list(range(8)))
    R = res.results
    B = 8
    y_prompt = np.stack([R[b]['yp'] for b in range(B)]).astype(np.float32)
    y_sample = np.concatenate([R[b]['ys'] for b in range(B)], 0).reshape(128, 1, D).astype(np.float32)
    lh_p = np.stack([R[b]['o_lh_p'].reshape(D) for b in range(B)])[None].astype(np.float32)
    lc_p = np.stack([R[b]['o_lc_p'] for b in range(B)])[None].astype(np.float32)
    cc_p = np.stack([R[b]['o_cc_p'] for b in range(B)])[None].astype(np.float32)
    fc_p = np.stack([R[b]['o_fc_p'] for b in range(B)], 1).astype(np.float32)
    mk = np.stack([R[b]['o_mk'] for b in range(B)], 1).reshape(2, B, NMEM, 4, 256).astype(np.float32)
    mv = np.stack([R[b]['o_mv'] for b in range(B)], 1).reshape(2, B, NMEM, 4, 256).astype(np.float32)
    lh_s = np.concatenate([R[b]['o_lh_s'] for b in range(B)], 0)[None].astype(np.float32)
    lc_s = np.concatenate([R[b]['o_lc_s'] for b in range(B)], 0)[None].astype(np.float32)
    cc_s = np.concatenate([R[b]['o_cc_s'] for b in range(B)], 0)[None].astype(np.float32)
    fc_s = np.concatenate([R[b]['o_fc_s'] for b in range(B)], 1).astype(np.float32)
    return (y_prompt, y_sample, lh_p, lc_p, cc_p, fc_p, mk, mv, lh_s, lc_s, cc_s, fc_s)
```

```python
import contextlib
import numpy as np
import concourse.bass as bass
import concourse.mybir as mybir
from concourse.bass_utils import run_bass_kernel_spmd

F32 = mybir.dt.float32
BF16 = mybir.dt.bfloat16
AF = mybir.ActivationFunctionType
ALU = mybir.AluOpType

D = 1024
T = 2048
NT = 512
NS = 16
DFF = 3072
NMEM = 256
EPS = 1e-6
SEM_MAX = 30000
NDMASEM = 12
GC1 = 0.044715 ** 0.5
GC2 = 0.7978845608028654

_PCOLS = [
    ('norm_mix', 16), ('norm_xa', 16), ('norm_ffn', 16), ('norm_final', 8), ('mem_norm', 16),
    ('lru_conv_w', 32), ('lru_conv_b', 8), ('lru_ba', 8), ('lru_bx', 8), ('lru_lambda', 8),
    ('cm_b_pw1', 16), ('cm_dw_w', 248), ('cm_dw_b', 8), ('cm_ln_g', 8), ('cm_ln_b', 8),
    ('ffn_conv_w', 144), ('ffn_conv_b', 48),
]
POFF = {}
_o = 0
for _n, _c in _PCOLS:
    POFF[_n] = _o
    _o += _c
NPAR = _o


def _cols(v):
    return np.ascontiguousarray(np.asarray(v, np.float32).reshape(-1, 128).T)


def pack_params(inp):
    out = np.zeros((128, NPAR), np.float32)

    def put(name, arr):
        a = _cols(arr)
        out[:, POFF[name]:POFF[name] + a.shape[1]] = a

    for n in ('norm_mix', 'norm_xa', 'norm_ffn', 'norm_final', 'mem_norm', 'lru_conv_w', 'lru_conv_b',
              'lru_ba', 'lru_bx', 'lru_lambda', 'cm_b_pw1', 'cm_dw_b', 'cm_ln_g', 'cm_ln_b',
              'ffn_conv_w', 'ffn_conv_b'):
        put(n, inp[n])
    w = np.asarray(inp['cm_dw_w'], np.float32)[0]
    w = w.reshape(31, 8, 128).transpose(2, 1, 0).reshape(128, 248)
    out[:, POFF['cm_dw_w']:POFF['cm_dw_w'] + 248] = w
    return out


class Buf:
    __slots__ = ('name', 'w', 'r')

    def __init__(self, name):
        self.name = name
        self.w = None
        self.r = {}


class Tracker:
    ENGS = ('pe', 'act', 'dve', 'pool', 'sp')

    def __init__(self, nc, stack):
        self.nc = nc
        self.stack = stack
        self.ops = {e: [] for e in self.ENGS}
        self.cur = {}
        self.known = {e: {} for e in self.ENGS}
        self.nsem = 0
        self.dpool = {}
        self.drr = {}
        self.own_pe = set()

    def new_sem(self, tag):
        self.nsem += 1
        return self.stack.enter_context(self.nc.semaphore(f"{tag}{self.nsem}"))

    def _event(self, eng):
        c = self.cur.get(eng)
        if c is None or c[1] >= SEM_MAX:
            c = [self.new_sem('s' + eng), 0]
            self.cur[eng] = c
            if eng == 'pe':
                self.own_pe.add(id(c[0]))
        c[1] += 1
        return (c[0], c[1])

    def _collect(self, eng, reads, writes, extra=()):
        need = {}
        kn = self.known[eng]

        def add(ev):
            if ev is None:
                return
            s, v = ev
            k = id(s)
            if eng == 'pe' and k in self.own_pe:
                return
            if kn.get(k, 0) >= v:
                return
            if k not in need or need[k][1] < v:
                need[k] = (s, v)

        for b in reads:
            add(b.w)
        for b in writes:
            add(b.w)
            for ev in b.r.values():
                add(ev)
        for ev in extra:
            add(ev)
        for k, (s, v) in need.items():
            kn[k] = v
        return list(need.values())

    def _commit(self, ev, reads, writes):
        for b in writes:
            b.w = ev
            b.r = {}
        for b in reads:
            if b in writes:
                continue
            b.r[id(ev[0])] = ev

    def op(self, eng, fn, reads=(), writes=(), r=None, w=None):
        reads = r if r is not None else reads
        writes = w if w is not None else writes
        waits = self._collect(eng, reads, writes)
        ev = self._event(eng)
        self.ops[eng].append((waits, fn, ev[0], 1))
        self._commit(ev, reads, writes)

    def dma(self, eng, fn, reads=(), writes=(), r=None, w=None):
        reads = r if r is not None else reads
        writes = w if w is not None else writes
        pool = self.dpool.setdefault(eng, [])
        i = self.drr.get(eng, 0)
        self.drr[eng] = i + 1
        if len(pool) < NDMASEM:
            pool.append([self.new_sem('d' + eng), 0, None])
        slot = pool[i % NDMASEM]
        waits = self._collect(eng, reads, writes, extra=[slot[2]] if slot[2] else [])
        slot[1] += 16
        ev = (slot[0], slot[1])
        slot[2] = ev
        self.ops[eng].append((waits, fn, slot[0], 16))
        self._commit(ev, reads, writes)

    def finish(self):
        evs = []
        for eng, pool in self.dpool.items():
            for slot in pool:
                if slot[2]:
                    evs.append(slot[2])
        for eng, c in self.cur.items():
            evs.append((c[0], c[1]))
        waits = self._collect('sp', (), (), extra=evs)
        self.ops['sp'].append((waits, None, None, 0))

    def emit(self, block):
        decs = {'pe': block.tensor, 'act': block.scalar, 'dve': block.vector, 'pool': block.gpsimd,
                'sp': block.sync}
        for e in self.ENGS:
            ops = self.ops[e]

            def body(eng, ops=ops):
                for waits, fn, sem, amt in ops:
                    for s, v in waits:
                        eng.wait_ge(s, v)
                    if fn is None:
                        continue
                    ins = fn(eng)
                    ins.then_inc(sem, amt)

            decs[e](body)


class _Stop(Exception):
    pass


def build_program(stage=None):
    nc = bass.Bass("TRN2", target_bir_lowering=False)

    def din(name, shape):
        return nc.dram_tensor(name, list(shape), F32, kind="ExternalInput").ap()

    def dout(name, shape):
        return nc.dram_tensor(name, list(shape), F32, kind="ExternalOutput").ap()

    xp = din("xp", [T, D]); xs = din("xs", [NS, D])
    st_h = din("st_h", [NS, D]); st_lc = din("st_lc", [NS * 3, D]); st_cc = din("st_cc", [NS * 30, D])
    st_fc = din("st_fc", [2, NS * 2, DFF])
    ck = din("ck", [2, NS, NMEM, D]); cv = din("cv", [2, NS, NMEM, D])
    memp = din("memp", [NMEM, D])
    params = din("params", [128, NPAR]); ident_d = din("ident", [128, 128])
    w_in = din("lru_w_in", [D, 2 * D]); w_a = din("lru_wa", [4, 256, 256]); w_x = din("lru_wx", [4, 256, 256])
    w_out = din("lru_w_out", [D, D])
    w_pw1 = din("cm_w_pw1", [D, 2 * D]); w_pw2 = din("cm_w_pw2", [D, D])
    w_q = din("xa_w_q", [2, D, D]); w_kv = din("xa_w_kv", [2, D, 2 * D]); w_o = din("xa_w_o", [2, D, D])
    w_up = din("ffn_w_up", [2, D, 2 * DFF]); w_dn = din("ffn_w_down", [2, DFF, D])

    yp = dout("yp", [T, D]); ys = dout("ys", [NS, D])
    o_lh_p = dout("o_lh_p", [1, D]); o_lc_p = dout("o_lc_p", [3, D]); o_cc_p = dout("o_cc_p", [30, D])
    o_fc_p = dout("o_fc_p", [2, 2, DFF])
    o_mk = dout("o_mk", [2, NMEM, D]); o_mv = dout("o_mv", [2, NMEM, D])
    o_lh_s = dout("o_lh_s", [NS, D]); o_lc_s = dout("o_lc_s", [NS, 3, D]); o_cc_s = dout("o_cc_s", [NS, 30, D])
    o_fc_s = dout("o_fc_s", [2, NS, 2, DFF])

    with contextlib.ExitStack() as stack:
        Tk = Tracker(nc, stack)

        def ckpt(n):
            if stage is not None and n == stage:
                raise _Stop()

        def sb(name, shape, dt):
            return stack.enter_context(nc.sbuf_tensor(name, list(shape), dt))

        XP = sb("XP", [128, 8, NT], F32); XS = sb("XS", [128, 8, NS], F32)
        HNP = sb("HNP", [128, 8, NT], BF16); HNS = sb("HNS", [128, 8, NS], BF16)
        RING = [sb(f"RING{i}", [128, 8192], BF16) for i in range(4)]
        WA = sb("WA", [128, 8, NT], BF16); WB = sb("WB", [128, 8, NT], BF16)
        FB = [sb(f"F{i}", [128, 2, 520], F32) for i in range(6)]
        RSTD = sb("RSTD", [128, NT], F32); LNV = sb("LNV", [128, NT], F32)
        WBS = sb("WBS", [128, 8, NS], BF16); WAS = sb("WAS", [128, 8, NS], BF16); WBS2 = sb("WBS2", [128, 8, NS], BF16)
        CVS = sb("CVS", [128, 8, NS], F32)
        PT = sb("PT", [128, 2, NT], BF16); PT2 = sb("PT2", [128, 2, NT], BF16)
        GG2 = sb("GG2", [128, 2, 520], F32); RC2 = sb("RC2", [128, 2, 520], F32)
        KT = sb("KT", [128, 2, 8, NMEM], BF16); VV = sb("VV", [128, 2, 2, D], BF16)
        PAR = sb("PAR", [128, NPAR], F32)
        DER = sb("DER", [128, 48], F32)
        CST = sb("CST", [128, 8], F32)
        IDF = sb("IDF", [128, 128], F32); IDB = sb("IDB", [128, 128], BF16); ONESB = sb("ONESB", [128, 128], BF16)
        DG = sb("DG", [128, 31, 128], BF16)
        KS = sb("KS", [128, 2, D], BF16); VS = sb("VS", [128, 2, D], BF16)
        RECH = sb("RECH", [128, 8, 3], F32); HST = sb("HST", [128, 8], F32)
        GHALO = sb("GHALO", [128, 8, 30], BF16); GST = sb("GST", [128, 8, 30], F32)
        FHALO = sb("FHALO", [128, 2, 24, 2], F32)
        RECS = sb("RECS", [128, 8, NS, 4], F32); H0S = sb("H0S", [128, 8, NS], F32)
        GBS = sb("GBS", [128, 8, NS, 31], BF16); GNS = sb("GNS", [128, 8, NS], F32)
        FSS = sb("FSS", [128, 24, NS, 3], F32)
        QS = sb("QS", [NS, D], BF16); SCORE = sb("SCORE", [128, 8], F32); PB = sb("PB", [128, 8], BF16)
        RSS = sb("RSS", [128, 4, NS], F32)
        SMALL = sb("SMALL", [128, 64], F32)
        PSUM = stack.enter_context(nc.psum_tensor("PSUM", [128, 8, 512], F32))

        bXP = [Buf(f"xp{c}") for c in range(8)]; bXS = [Buf(f"xs{c}") for c in range(8)]
        bHNP = Buf("hnp"); bHNS = Buf("hns")
        bRING = [Buf(f"ring{i}") for i in range(4)]
        bWA = Buf("wa"); bWB = Buf("wb"); bF = [Buf(f"f{i}") for i in range(6)]
        bWBh = [Buf("wbh0"), Buf("wbh1")]; bWBS = [Buf("wbs0"), Buf("wbs1")]
        bIOB = [Buf("iob0"), Buf("iob1")]
        bWBq = [Buf(f"wbq{i}") for i in range(4)]
        bWAS = Buf("was"); bWBS2 = Buf("wbs2"); bCVS = Buf("cvs")
        bLRU = {k: [Buf(k + '0'), Buf(k + '1')] for k in ('ta', 'tx', 'aa', 'mm', 'ta2', 'tx2', 'aa2')}
        for k in ('gg', 'rc', 'pt'):
            bLRU[k] = [[Buf(k + '00'), Buf(k + '01')], [Buf(k + '10'), Buf(k + '11')], [bWB, bWB]]
        bGBp = [Buf("gbp0"), Buf("gbp1")]; bTBp = [Buf("tbp0"), Buf("tbp1")]; bCVp = [Buf(f"cvp{i}") for i in range(8)]
        bGPp = [Buf("gpp0"), Buf("gpp1")]; bT1p = [Buf("t1p0"), Buf("t1p1")]; bT2p = [Buf("t2p0"), Buf("t2p1")]
        WAw = [bWA, bIOB[0], bIOB[1]]
        bPT2 = Buf("pt2"); bGG2 = Buf("gg2"); bRC2 = Buf("rc2")
        bRSTD = Buf("rstd"); bLNV = Buf("lnv"); bPT = Buf("pt"); bKT = Buf("kt"); bVV = Buf("vv")
        bMEMX = Buf("memx"); bMEMN = Buf("memn"); bPAR = Buf("par"); bDER = Buf("der"); bCST = Buf("cst")
        bID = Buf("id"); bSEL = Buf("sel"); bDG = Buf("dg"); bKS = Buf("ks"); bVS = Buf("vs")
        bRECH = Buf("rech"); bHST = Buf("hst"); bGHALO = Buf("ghalo"); bGST = Buf("gst"); bFHALO = Buf("fhalo")
        bRECS = Buf("recs"); bH0S = Buf("h0s"); bGBS = Buf("gbs"); bGNS = Buf("gns"); bFSS = Buf("fss")
        bQS = Buf("qs"); bSCORE = Buf("score"); bPB = Buf("pb"); bRSS = Buf("rss"); bSMALL = Buf("small")
        bPS = [Buf(f"ps{i}") for i in range(8)]
        psrr = [0]
        cur_g = [0]

        reserved = set()

        def bank():
            while True:
                i = psrr[0] % 8
                psrr[0] += 1
                if i not in reserved:
                    return i

        def act(out, in_, func, bias=None, scale=None, r=(), w=()):
            kw = {}
            if bias is not None:
                kw['bias'] = bias
            if scale is not None:
                kw['scale'] = scale
            Tk.op('act', lambda e: e.activation(out=out, in_=in_, func=func, **kw), r, w)

        def stt(out, in0, scalar, in1, op0, op1, r=(), w=(), accum=None, eng='dve'):
            Tk.op(eng, lambda e: e.scalar_tensor_tensor(out=out, in0=in0, scalar=scalar, in1=in1, op0=op0,
                                                       op1=op1, accum_out=accum), r, w)

        def ts(out, in0, s1, s2, op0, op1, r=(), w=(), eng='dve'):
            Tk.op(eng, lambda e: e.tensor_scalar(out=out, in0=in0, scalar1=s1, scalar2=s2, op0=op0, op1=op1), r, w)

        def tt(out, in0, in1, op, r=(), w=(), eng='dve'):
            Tk.op(eng, lambda e: e.tensor_tensor(out=out, in0=in0, in1=in1, op=op), r, w)

        def cp(out, in_, r=(), w=(), eng='dve'):
            if eng == 'act':
                Tk.op(eng, lambda e: e.activation(out=out, in_=in_, func=AF.Copy), r, w)
            else:
                Tk.op(eng, lambda e: e.tensor_copy(out=out, in_=in_), r, w)

        def mm(out, pairs, r=(), w=()):
            def fn(e):
                n = len(pairs)
                ins = None
                for i, (l, rh) in enumerate(pairs):
                    ins = e.matmul(out, l, rh, start=(i == 0), stop=(i == n - 1))
                return ins
            Tk.op('pe', fn, r, w)

        def tr(out, in_, r=(), w=()):
            npart = in_.shape[0]
            Tk.op('pe', lambda e: e.transpose(out, in_, IDF[0:npart, 0:npart]), tuple(r) + (bID,), w)

        def dma(eng, out, in_, r=(), w=()):
            Tk.dma(eng, lambda e: e.dma_start(out=out, in_=in_), r, w)

        def P(name, i=0, n=1):
            o = POFF[name] + i
            return PAR[:, o:o + n]

        dma('sp', PAR[:], params[:, :], w=[bPAR])
        dma('sp', IDF[:], ident_d[:, :], w=[bID])
        Tk.op('dve', lambda e: e.memset(ONESB[:], 1.0), w=[bID])
        Tk.op('dve', lambda e: e.memset(CST[:, 0:1], EPS), w=[bCST])
        Tk.op('dve', lambda e: e.memset(CST[:, 1:2], 1.0), w=[bCST])
        Tk.op('dve', lambda e: e.memset(CST[:, 2:3], 0.0), w=[bCST])
        C_EPS = CST[:, 0:1]; C_ONE = CST[:, 1:2]; C_ZERO = CST[:, 2:3]
        cp(IDB[:], IDF[:], r=[bID], w=[bID])
        Tk.op('dve', lambda e: e.memset(RECH[:], 0.0), w=[bRECH])
        Tk.op('dve', lambda e: e.memset(HST[:], 0.0), w=[bHST])
        Tk.op('dve', lambda e: e.memset(GHALO[:], 0.0), w=[bGHALO])
        Tk.op('dve', lambda e: e.memset(FHALO[:], 0.0), w=[bFHALO])
        ts(DER[:, 0:8], P('lru_ba', 0, 8), 0.5, None, ALU.mult, ALU.bypass, r=[bPAR], w=[bDER])
        ts(DER[:, 8:16], P('lru_bx', 0, 8), 0.5, None, ALU.mult, ALU.bypass, r=[bPAR], w=[bDER])
        act(SMALL[:, 0:8], P('lru_lambda', 0, 8), AF.Exp, scale=-1.0, r=[bPAR], w=[bSMALL])
        act(SMALL[:, 8:16], SMALL[:, 0:8], AF.Ln, bias=C_ONE, r=[bSMALL, bCST], w=[bSMALL])
        ts(DER[:, 16:24], SMALL[:, 8:16], -8.0, None, ALU.mult, ALU.bypass, r=[bSMALL], w=[bDER])
        ts(DER[:, 24:32], SMALL[:, 8:16], -4.0, None, ALU.mult, ALU.bypass, r=[bSMALL], w=[bDER])
        ts(DER[:, 32:40], P('cm_b_pw1', 8, 8), 0.5, None, ALU.mult, ALU.bypass, r=[bPAR], w=[bDER])
        ts(DER[:, 40:48], P('cm_ln_b', 0, 8), 0.5, None, ALU.mult, ALU.bypass, r=[bPAR], w=[bDER])

        def load_unit(i, parts):
            for off, shp, src in parts:
                n = int(np.prod(shp))
                dst = RING[i][:, off:off + n]
                if len(shp) == 2:
                    dst = dst.rearrange("p (a b) -> p a b", a=shp[0])
                elif len(shp) == 3:
                    dst = dst.rearrange("p (a b c) -> p a b c", a=shp[0], b=shp[1])
                dma('pool', dst, src, w=[bRING[i]])

        def wview(i, off, shp):
            n = int(np.prod(shp))
            v = RING[i][:, off:off + n]
            if len(shp) == 2:
                v = v.rearrange("p (a b) -> p a b", a=shp[0])
            elif len(shp) == 3:
                v = v.rearrange("p (a b c) -> p a b c", a=shp[0], b=shp[1])
            return v

        def kxn(dram2d):
            return dram2d.rearrange("(k p) n -> p k n", p=128)

        def unit_list():
            ul = []
            for li in range(2):
                ul.append((f'wk{li}', [(0, [8, 1024], kxn(w_kv[li, :, 0:1024]))]))
                ul.append((f'wv{li}', [(0, [8, 1024], kxn(w_kv[li, :, 1024:2048]))]))
            for g in range(4):
                for li in range(2):
                    if li == 0:
                        ul.append((f'{g}win_g', [(0, [8, 1024], kxn(w_in[:, 0:1024]))]))
                        ul.append((f'{g}win_r', [(0, [8, 1024], kxn(w_in[:, 1024:2048]))]))
                        ul.append((f'{g}wax', [(0, [4, 2, 256], w_a.rearrange("n (j p) d -> p n j d", p=128)),
                                               (2048, [4, 2, 256], w_x.rearrange("n (j p) d -> p n j d", p=128))]))
                        ul.append((f'{g}wout', [(0, [8, 1024], kxn(w_out[:, :]))]))
                    else:
                        ul.append((f'{g}pw1a', [(0, [8, 1024], kxn(w_pw1[:, 0:1024]))]))
                        ul.append((f'{g}pw1b', [(0, [8, 1024], kxn(w_pw1[:, 1024:2048]))]))
                        ul.append((f'{g}pw2', [(0, [8, 1024], kxn(w_pw2[:, :]))]))
                    ul.append((f'{g}wq{li}', [(0, [8, 1024], kxn(w_q[li, :, :]))]))
                    ul.append((f'{g}wo{li}', [(0, [8, 1024], kxn(w_o[li, :, :]))]))
                    for pc in range(6):
                        ul.append((f'{g}up{li}_{pc}',
                                   [(0, [8, 512], kxn(w_up[li, :, pc * 512:(pc + 1) * 512])),
                                    (4096, [8, 512], kxn(w_up[li, :, DFF + pc * 512:DFF + (pc + 1) * 512]))]))
                        ul.append((f'{g}dn{li}_{pc}', [(0, [4, 1024], kxn(w_dn[li, pc * 512:(pc + 1) * 512, :]))]))
            return ul

        class Units:
            def __init__(self, ul):
                self.ul = ul
                self.names = [n for n, _ in ul]
                self.pos = 0
                self.slot = {}
                self.free = [0, 1, 2, 3]
                self.pfx = ''

            def _fill(self):
                while self.pos < len(self.ul) and self.free:
                    name, parts = self.ul[self.pos]
                    i = self.free.pop(0)
                    load_unit(i, parts)
                    self.slot[name] = i
                    self.pos += 1

            def get(self, name):
                name = self.pfx + name if (self.pfx + name) in self.names else name
                idx = self.names.index(name)
                self._fill()
                assert idx < self.pos, (name, "ring full")
                return self.slot[name]

            def rel(self, name):
                name = self.pfx + name if (self.pfx + name) in self.names else name
                self.free.append(self.slot[name])
                self._fill()

        class TL:
            pass
        tp = TL(); tp.N = NT; tp.X = XP; tp.HN = HNP; tp.xb = bXP; tp.hb = bHNP; tp.samp = False
        tsm = TL(); tsm.N = NS; tsm.X = XS; tsm.HN = HNS; tsm.xb = bXS; tsm.hb = bHNS; tsm.samp = True

        IOB = WA[:].bitcast(F32).rearrange("p (a b) c -> p a (b c)", a=2)

        def load_T(src_rows, nrows, dstfn, wbufs, half=0, inre=None):
            dma('sp', IOB[0:nrows, half, :], src_rows, w=[bIOB[half], bWA])
            for c0 in range(0, 8, 4):
                b = bank()
                for j in range(4):
                    c = c0 + j
                    tr(PSUM[:, b, j * 128:j * 128 + nrows], IOB[0:nrows, half, c * 128:(c + 1) * 128],
                       r=[bIOB[half]], w=[bPS[b]])
                for j in range(4):
                    c = c0 + j
                    wb = wbufs[c] if isinstance(wbufs, list) else wbufs
                    src_ap = PSUM[:, b, j * 128:j * 128 + nrows]
                    if inre is not None:
                        src_ap = inre(src_ap)
                    Tk.op('act', lambda e, c=c, src_ap=src_ap: e.activation(out=dstfn(c), in_=src_ap, func=AF.Copy),
                          [bPS[b]], [wb])

        st_half = [0]

        def store_T(srcfn, ncols, dst_rows, rbufs):
            half = st_half[0] % 2
            st_half[0] += 1
            hb_ = bIOB[half]
            for c0 in range(0, 8, 4):
                b = bank()
                rb = list({id(x): x for x in ([rbufs[c0 + j] for j in range(4)] if isinstance(rbufs, list) else [rbufs])}.values())

                def fnT(e, b=b, c0=c0):
                    ins = None
                    for j in range(4):
                        ins = e.transpose(PSUM[0:ncols, b, j * 128:(j + 1) * 128], srcfn(c0 + j), IDF[:, :])
                    return ins
                Tk.op('pe', fnT, rb + [bID], [bPS[b]])
                cp(IOB[0:ncols, half, c0 * 128:(c0 + 4) * 128], PSUM[0:ncols, b, :], r=[bPS[b]], w=[hb_, bWA], eng='act')
            dma('sp', dst_rows, IOB[0:ncols, half, :], r=[hb_])

        def rmsnorm(tl, gname, gi, dst=None, dstb=None, xsrc=None, xbufs=None, N=None, out_f32=None):
            N = N or tl.N
            X = xsrc if xsrc is not None else tl.X
            xb = xbufs if xbufs is not None else tl.xb
            dst = dst if dst is not None else tl.HN
            dstb = dstb if dstb is not None else tl.hb
            b = bank()
            for q in range(4):
                act(WB[:, 2 * q:2 * q + 2, 0:N], X[:, 2 * q:2 * q + 2, 0:N], AF.Square,
                    r=list(xb[2 * q:2 * q + 2]), w=([bWBq[q], bWB] if q == 0 else [bWBq[q]]))

                def fnq(e, q=q, b=b, N=N):
                    ins = None
                    for k in (2 * q, 2 * q + 1):
                        ins = e.matmul(PSUM[:, b, 0:N], ONESB[:], WB[:, k, 0:N], start=(k == 0), stop=(k == 7))
                    return ins
                Tk.op('pe', fnq, [bWBq[q], bWB, bID], [bPS[b]])
            act(LNV[:, 0:N], PSUM[:, b, 0:N], AF.Ln, bias=C_EPS, scale=1.0 / D, r=[bPS[b], bCST], w=[bLNV])
            act(PSUM[:, b, 0:N], LNV[:, 0:N], AF.Exp, scale=-0.5, r=[bLNV], w=[bPS[b]])
            for c in range(8):
                o = out_f32(c) if out_f32 is not None else dst[:, c, 0:N]
                stt(o, X[:, c, 0:N], P(gname, gi * 8 + c), PSUM[:, b, 0:N], ALU.mult, ALU.mult,
                    r=[xb[c], bPS[b], bPAR], w=[dstb])

        def proj_residual(tl, wv, src, srcb, ringb, nk, scale=None, koff=0):
            N = tl.N
            for co in range(8):
                b = bank()
                mm(PSUM[:, b, 0:N], [(wv[:, k, co * 128:(co + 1) * 128], src[:, koff + k, 0:N]) for k in range(nk)],
                   r=[srcb, ringb], w=[bPS[b]])
                if scale is None:
                    tt(tl.X[:, co, 0:N], tl.X[:, co, 0:N], PSUM[:, b, 0:N], ALU.add, r=[bPS[b]], w=[tl.xb[co]])
                else:
                    stt(tl.X[:, co, 0:N], PSUM[:, b, 0:N], scale, tl.X[:, co, 0:N], ALU.mult, ALU.add,
                        r=[bPS[b]], w=[tl.xb[co]])

        def gelu2(out_ap, pre, t1, t2, r, wt1, wt2, wout):
            act(t1, pre, AF.Square, scale=GC1, r=r, w=[wt1])
            stt(t2, t1, 1.0, pre, ALU.add, ALU.mult, r=list(r) + [wt1], w=[wt2])
            act(t1, t2, AF.Tanh, scale=GC2, r=[wt2], w=[wt1])
            stt(out_ap, t1, 1.0, pre, ALU.add, ALU.mult, r=list(r) + [wt1], w=[wout])

        def lru_sublayer(tl, U, rel):
            N = tl.N
            rmsnorm(tl, 'norm_mix', 0)
            ig = U.get('win_g'); ir = U.get('win_r'); ia = U.get('wax'); io = U.get('wout')
            Wg = wview(ig, 0, [8, 1024]); Wr = wview(ir, 0, [8, 1024])
            Wa = wview(ia, 0, [4, 2, 256]); Wx = wview(ia, 2048, [4, 2, 256]); Wo = wview(io, 0, [8, 1024])
            WB32 = WB[:].bitcast(F32).rearrange("p (a b) c -> p a (b c)", a=4)
            GGs = [FB[0], GG2, WB32[:, 0:2, :]]; bGGs = bLRU['gg']
            RCs = [FB[1], RC2, WB32[:, 2:4, :]]; bRCs = bLRU['rc']
            PTs = [PT, PT2]; bPTs = bLRU['pt']
            MM = FB[5]
            bMMj = bLRU['mm']
            TAs = [FB[2], KS[:].bitcast(F32)]
            TXs = [FB[3], VS[:].bitcast(F32)]
            AAs = [FB[4], FSS[:].rearrange("p a b c -> p (a b c)")[:, 0:1024].rearrange("p (j n) -> p j n", j=2)]
            bTAs = [bLRU['ta'], bLRU['ta2']]; bTXs = [bLRU['tx'], bLRU['tx2']]; bAAs = [bLRU['aa'], bLRU['aa2']]

            def front(n):
                GG = GGs[n % 3]; bGGj = bGGs[n % 3]; RC = RCs[n % 3]; bRCj = bRCs[n % 3]
                PTn = PTs[n % 2]; bPTnj = bPTs[n % 2]
                bgs = []; brs = []
                for j in range(2):
                    c = 2 * n + j
                    bg = bank()
                    mm(PSUM[:, bg, 0:N], [(Wg[:, k, c * 128:(c + 1) * 128], tl.HN[:, k, 0:N]) for k in range(8)],
                       r=[tl.hb, bRING[ig]], w=[bPS[bg]])
                    br = bank()
                    mm(PSUM[:, br, 0:N], [(Wr[:, k, c * 128:(c + 1) * 128], tl.HN[:, k, 0:N]) for k in range(8)],
                       r=[tl.hb, bRING[ir]], w=[bPS[br]])
                    bgs.append(bg); brs.append(br)
                srcs = []
                for j in range(2):
                    c = 2 * n + j
                    bg = bgs[j]; br = brs[j]
                    act(GG[:, j, 0:N], PSUM[:, bg, 0:N], AF.Gelu_apprx_tanh, r=[bPS[bg]], w=[bGGj[j]])
                    if not tl.samp:
                        cp(MM[:, j, 0:3], RECH[:, c, :], r=[bRECH], w=[bMMj[j]], eng='dve')
                        cp(MM[:, j, 3:3 + N], PSUM[:, br, 0:N], r=[bPS[br]], w=[bMMj[j]], eng='act')
                        cp(RECH[:, c, :], MM[:, j, N:N + 3], r=[bMMj[j]], w=[bRECH], eng='dve')
                        srcs.append((lambda k, j=j: MM[:, j, k:k + N], bMMj[j]))
                    else:
                        cp(RECS[:, c, :, 3], PSUM[:, br, 0:N], r=[bPS[br]], w=[bRECS], eng='act')
                        srcs.append((lambda k, c=c: RECS[:, c, :, k], bRECS))
                for k in range(4):
                    for j in range(2):
                        c = 2 * n + j
                        src, srcb = srcs[j]
                        if k == 0:
                            ts(RC[:, j, 0:N], src(0), P('lru_conv_w', c), P('lru_conv_b', c), ALU.mult, ALU.add,
                               r=[srcb, bPAR], w=[bRCj[j]])
                        else:
                            stt(RC[:, j, 0:N], src(k), P('lru_conv_w', k * 8 + c), RC[:, j, 0:N], ALU.mult, ALU.add,
                                r=[srcb, bPAR], w=[bRCj[j]])
                for j in range(2):
                    cp(PTn[:, j, 0:N], RC[:, j, 0:N], r=[bRCj[j]], w=[bPTnj[j]], eng='pool')

            def back(n):
                GG = GGs[n % 3]; bGGj = bGGs[n % 3]; RC = RCs[n % 3]; bRCj = bRCs[n % 3]
                PTn = PTs[n % 2]; bPTnj = bPTs[n % 2]
                TA = TAs[n % 2]; TX = TXs[n % 2]; AA = AAs[n % 2]
                bTAj = bTAs[n % 2]; bTXj = bTXs[n % 2]; bAAj = bAAs[n % 2]
                bas = []; bxs = []
                for j in range(2):
                    ba_ = bank()
                    mm(PSUM[:, ba_, 0:N], [(Wa[:, n, jk, j * 128:(j + 1) * 128], PTn[:, jk, 0:N]) for jk in range(2)],
                       r=bPTnj + [bRING[ia]], w=[bPS[ba_]])
                    bx_ = bank()
                    mm(PSUM[:, bx_, 0:N], [(Wx[:, n, jk, j * 128:(j + 1) * 128], PTn[:, jk, 0:N]) for jk in range(2)],
                       r=bPTnj + [bRING[ia]], w=[bPS[bx_]])
                    bas.append(ba_); bxs.append(bx_)
                for j in range(2):
                    c = 2 * n + j
                    act(TA[:, j, 0:N], PSUM[:, bas[j], 0:N], AF.Tanh, bias=DER[:, c:c + 1], scale=0.5,
                        r=[bPS[bas[j]], bDER], w=[bTAj[j]])
                    act(TX[:, j, 0:N], PSUM[:, bxs[j], 0:N], AF.Tanh, bias=DER[:, 8 + c:9 + c], scale=0.5,
                        r=[bPS[bxs[j]], bDER], w=[bTXj[j]])
                for j in range(2):
                    c = 2 * n + j
                    act(AA[:, j, 0:N], TA[:, j, 0:N], AF.Exp, bias=DER[:, 24 + c:25 + c], scale=DER[:, 24 + c:25 + c],
                        r=[bTAj[j], bDER], w=[bAAj[j]])
                    act(TA[:, j, 0:N], TA[:, j, 0:N], AF.Exp, bias=DER[:, 16 + c:17 + c], scale=DER[:, 16 + c:17 + c],
                        r=[bDER], w=[bTAj[j]])
                for j in range(2):
                    act(TA[:, j, 0:N], TA[:, j, 0:N], AF.Ln, bias=C_ONE, scale=-1.0, r=[bCST], w=[bTAj[j]])
                for j in range(2):
                    act(TA[:, j, 0:N], TA[:, j, 0:N], AF.Exp, scale=0.5, w=[bTAj[j]])

            def back_b(n):
                GG = GGs[n % 3]; bGGj = bGGs[n % 3]; RC = RCs[n % 3]; bRCj = bRCs[n % 3]
                TA = TAs[n % 2]; TX = TXs[n % 2]; AA = AAs[n % 2]
                bTAj = bTAs[n % 2]; bTXj = bTXs[n % 2]; bAAj = bAAs[n % 2]
                for j in range(2):
                    stt(TX[:, j, 0:N], TX[:, j, 0:N], 1.0, TA[:, j, 0:N], ALU.add, ALU.mult, r=[bTAj[j]], w=[bTXj[j]])
                for j in range(2):
                    stt(TX[:, j, 0:N], TX[:, j, 0:N], 0.5, RC[:, j, 0:N], ALU.mult, ALU.mult, r=[bRCj[j]], w=[bTXj[j]])
                for j in range(2):
                    c = 2 * n + j
                    if not tl.samp:
                        Tk.op('dve', lambda e, j=j, c=c: e.tensor_tensor_scan(
                            out=TA[:, j, 0:N], data0=AA[:, j, 0:N], data1=TX[:, j, 0:N], initial=HST[:, c:c + 1],
                            op0=ALU.mult, op1=ALU.add), [bAAj[j], bTXj[j], bHST], [bTAj[j]])
                        cp(HST[:, c:c + 1], TA[:, j, N - 1:N], r=[bTAj[j]], w=[bHST], eng='dve')
                    else:
                        tt(TA[:, j, 0:N], AA[:, j, 0:N], H0S[:, c, :], ALU.mult, r=[bAAj[j], bH0S], w=[bTAj[j]])
                        tt(TA[:, j, 0:N], TA[:, j, 0:N], TX[:, j, 0:N], ALU.add, r=[bTXj[j]], w=[bTAj[j]])
                        cp(H0S[:, c, :], TA[:, j, 0:N], r=[bTAj[j]], w=[bH0S], eng='act')
                for j in range(2):
                    c = 2 * n + j
                    tt(WA[:, c, 0:N], TA[:, j, 0:N], GG[:, j, 0:N], ALU.mult, r=[bTAj[j], bGGj[j]], w=WAw)

            for it in range(5):
                if it < 4:
                    front(it)
                if it >= 1:
                    back(it - 1)
                    back_b(it - 1)
            if rel:
                U.rel('win_g'); U.rel('win_r'); U.rel('wax')
            proj_residual(tl, Wo, WA, bWA, bRING[io], 8)
            if rel:
                U.rel('wout')

        def cmod_sublayer(tiles, U):
            for tl in tiles:
                rmsnorm(tl, 'norm_mix', 1)
            ia = U.get('pw1a'); ib = U.get('pw1b'); i2 = U.get('pw2')
            Wa_ = wview(ia, 0, [8, 1024]); Wb_ = wview(ib, 0, [8, 1024]); W2 = wview(i2, 0, [8, 1024])
            GBUF = FB[0][:].bitcast(BF16)
            TB = FB[1]
            CVp = [FB[2], FB[3], FB[4], FB[5]]
            st = {}

            def cv_of(tl, c):
                if tl.samp:
                    return CVS[:, c, :], bCVS
                return CVp[c // 2][:, c % 2, 0:tl.N], bCVp[c]

            def lnbufs(tl):
                if tl.samp:
                    return WAS, [bWAS], bWAS, WBS2, bWBS2
                return WA, WAw, bWA, WB, bWB

            def build_dg(c):
                o = POFF['cm_dw_w'] + c * 31
                if cur_g[0] >= 1 and c % 2 == 1:
                    DGc = GBS[:].rearrange("p a b c -> p (a b c)").rearrange("p (k q) -> p k q", k=31)
                    bDGc = bGBS
                else:
                    DGc = DG[:]
                    bDGc = bDG
                Tk.op('pool', lambda e, o=o, DGc=DGc: e.tensor_tensor(
                    out=DGc, in0=IDF[:].unsqueeze(1).broadcast_to([128, 31, 128]),
                    in1=PAR[:, o:o + 31].unsqueeze(2).broadcast_to([128, 31, 128]), op=ALU.mult), [bID, bPAR], [bDGc])
                return DGc, bDGc

            def stage_a(tl, c, dg):
                N = tl.N
                q = c % 2
                bGB = bGBp[q]; bTB = bTBp[q]
                DGc, bDGc = dg
                pa = bank()
                mm(PSUM[:, pa, 0:N], [(Wa_[:, k, c * 128:(c + 1) * 128], tl.HN[:, k, 0:N]) for k in range(8)],
                   r=[tl.hb, bRING[ia]], w=[bPS[pa]])
                pb = bank()
                mm(PSUM[:, pb, 0:N], [(Wb_[:, k, c * 128:(c + 1) * 128], tl.HN[:, k, 0:N]) for k in range(8)],
                   r=[tl.hb, bRING[ib]], w=[bPS[pb]])
                act(TB[:, q, 0:N], PSUM[:, pb, 0:N], AF.Sigmoid, bias=P('cm_b_pw1', 8 + c),
                    r=[bPS[pb], bPAR], w=[bTB])
                if not tl.samp:
                    cp(GBUF[:, q, 0:30], GHALO[:, c, :], r=[bGHALO], w=[bGB], eng='act')
                    stt(GBUF[:, q, 30:30 + N], PSUM[:, pa, 0:N], P('cm_b_pw1', c), TB[:, q, 0:N], ALU.add, ALU.mult,
                        r=[bTB, bPS[pa], bPAR], w=[bGB])
                    cp(GHALO[:, c, :], GBUF[:, q, N:N + 30], r=[bGB], w=[bGHALO], eng='act')
                    stt(GST[:, c, :], PSUM[:, pa, N - 30:N], P('cm_b_pw1', c), TB[:, q, N - 30:N], ALU.add, ALU.mult,
                        r=[bTB, bPS[pa], bPAR], w=[bGST])
                    src = lambda k, q=q, N=N: GBUF[:, q, k:k + N]
                    srcb = bGB
                else:
                    stt(GBS[:, c, :, 30], PSUM[:, pa, 0:N], P('cm_b_pw1', c), TB[:, q, 0:N], ALU.add, ALU.mult,
                        r=[bTB, bPS[pa], bPAR], w=[bGBS])
                    stt(GNS[:, c, :], PSUM[:, pa, 0:N], P('cm_b_pw1', c), TB[:, q, 0:N], ALU.add, ALU.mult,
                        r=[bTB, bPS[pa], bPAR], w=[bGNS])
                    src = lambda k, c=c: GBS[:, c, :, k]
                    srcb = bGBS
                st[(id(tl), c)] = (DGc, bDGc, src, srcb)

            def stage_b(tl, c):
                N = tl.N
                DGc, bDGc, src, srcb = st[(id(tl), c)]
                WAx, WAxw, bWAx, WBx, bWBx = lnbufs(tl)
                pc_ = bank()
                mm(PSUM[:, pc_, 0:N], [(DGc[:, k, :], src(k)) for k in range(31)], r=[bDGc, srcb], w=[bPS[pc_]])
                cvc, bcv = cv_of(tl, c)
                act(cvc, PSUM[:, pc_, 0:N], AF.Identity, bias=P('cm_dw_b', c), r=[bPS[pc_], bPAR], w=[bcv])
                cp(WAx[:, c, 0:N], cvc, r=[bcv], w=WAxw, eng='act')
                act(WBx[:, c, 0:N], cvc, AF.Square, r=[bcv], w=[bWBx])

            if cur_g[0] == 0:
                for c in range(8):
                    dg = build_dg(c)
                    for tl in tiles:
                        stage_a(tl, c, dg)
                    for tl in tiles:
                        stage_b(tl, c)
            else:
                for it in range(9):
                    if it < 8:
                        dg = build_dg(it)
                        for tl in tiles:
                            stage_a(tl, it, dg)
                    if it >= 1:
                        for tl in tiles:
                            stage_b(tl, it - 1)
            U.rel('pw1a'); U.rel('pw1b')
            Z = FB[0]
            for tl in tiles:
                N = tl.N
                WAx, WAxw, bWAx, WBx, bWBx = lnbufs(tl)
                p1 = bank()
                mm(PSUM[:, p1, 0:N], [(ONESB[:], WAx[:, k, 0:N]) for k in range(8)], r=[bWAx, bID], w=[bPS[p1]])
                p2 = bank()
                mm(PSUM[:, p2, 0:N], [(ONESB[:], WBx[:, k, 0:N]) for k in range(8)], r=[bWBx, bID], w=[bPS[p2]])
                Tk.op('act', lambda e, p1=p1, N=N: e.activation(out=LNV[:, 0:N], in_=PSUM[:, p1, 0:N], func=AF.Copy,
                                                               scale=1.0 / D), [bPS[p1]], [bLNV])
                act(RSTD[:, 0:N], LNV[:, 0:N], AF.Square, r=[bLNV], w=[bRSTD])
                stt(RSTD[:, 0:N], PSUM[:, p2, 0:N], 1.0 / D, RSTD[:, 0:N], ALU.mult, ALU.subtract, r=[bPS[p2]], w=[bRSTD])
                act(RSTD[:, 0:N], RSTD[:, 0:N], AF.Ln, bias=C_EPS, r=[bCST], w=[bRSTD])
                act(RSTD[:, 0:N], RSTD[:, 0:N], AF.Exp, scale=-0.5, w=[bRSTD])
                for c in range(8):
                    q = c % 2
                    bZ = bGBp[q]
                    cvc, bcv = cv_of(tl, c)
                    tt(Z[:, q, 0:N], cvc, LNV[:, 0:N], ALU.subtract, r=[bcv, bLNV], w=[bZ])
                    stt(Z[:, q, 0:N], Z[:, q, 0:N], P('cm_ln_g', c), RSTD[:, 0:N], ALU.mult, ALU.mult, r=[bRSTD, bPAR], w=[bZ])
                    act(WAx[:, c, 0:N], Z[:, q, 0:N], AF.Silu, bias=P('cm_ln_b', c), r=[bZ, bPAR], w=WAxw)
                proj_residual(tl, W2, WAx, bWAx, bRING[i2], 8)
            U.rel('pw2')

        def mem_kv(li, U):
            ik = U.get(f'wk{li}'); iv = U.get(f'wv{li}')
            Wk = wview(ik, 0, [8, 1024]); Wv = wview(iv, 0, [8, 1024])
            class M: pass
            m = M(); m.N = NMEM
            MEMN = HNP
            bMEMN = bHNP
            rmsnorm(m, 'mem_norm', li, dst=HNP, dstb=bHNP, xsrc=XP, xbufs=bXP, N=NMEM)
            for c in range(8):
                b = bank()
                mm(PSUM[:, b, 0:NMEM], [(Wk[:, k, c * 128:(c + 1) * 128], MEMN[:, k, 0:NMEM]) for k in range(8)],
                   r=[bMEMN, bRING[ik]], w=[bPS[b]])
                cp(KT[:, li, c, :], PSUM[:, b, 0:NMEM], r=[bPS[b]], w=[bKT], eng='act')
            for (Wm, ib_, odram, isv) in ((Wk, ik, o_mk, False), (Wv, iv, o_mv, True)):
                for mc in range(2):
                    ih = 0
                    for hf in range(2):
                        b = bank()
                        mm(PSUM[:, b, :], [(MEMN[:, k, mc * 128:(mc + 1) * 128], Wm[:, k, hf * 512:(hf + 1) * 512])
                                           for k in range(8)], r=[bMEMN, bRING[ib_]], w=[bPS[b]])
                        cp(IOB[:, ih, hf * 512:(hf + 1) * 512], PSUM[:, b, :], r=[bPS[b]], w=[bIOB[ih], bWA], eng='act')
                        if isv:
                            cp(VV[:, li, mc, hf * 512:(hf + 1) * 512], PSUM[:, b, :], r=[bPS[b]], w=[bVV], eng='dve')
                    dma('sp', odram[li, mc * 128:(mc + 1) * 128, :], IOB[:, ih, :], r=[bIOB[ih]])
            U.rel(f'wk{li}'); U.rel(f'wv{li}')

        def xattn_prompt(tl, li, U, rel):
            N = tl.N
            iq = U.get(f'wq{li}'); io = U.get(f'wo{li}')
            Wq = wview(iq, 0, [8, 1024]); Wo = wview(io, 0, [8, 1024])
            rmsnorm(tl, 'norm_xa', li)
            for c in range(8):
                b = bank()
                mm(PSUM[:, b, 0:N], [(Wq[:, k, c * 128:(c + 1) * 128], tl.HN[:, k, 0:N]) for k in range(8)],
                   r=[tl.hb, bRING[iq]], w=[bPS[b]])
                cp(WB[:, c, 0:N], PSUM[:, b, 0:N], r=[bPS[b]], w=[bWB], eng='act')
            PTs = [PT, PT2]; bPTs = [bPT, bPT2]

            def sphase(h):
                PTh = PTs[h % 2]; bPTh = bPTs[h % 2]
                for mc in range(2):
                    b = bank()
                    mm(PSUM[:, b, 0:N], [(KT[:, li, 2 * h + j, mc * 128:(mc + 1) * 128], WB[:, 2 * h + j, 0:N]) for j in range(2)],
                       r=[bKT, bWB], w=[bPS[b]])
                    act(PTh[:, mc, 0:N], PSUM[:, b, 0:N], AF.Exp, scale=1.0 / 16.0, r=[bPS[b]], w=[bPTh])

            def vphase(h):
                PTh = PTs[h % 2]; bPTh = bPTs[h % 2]
                bs = bank()
                mm(PSUM[:, bs, 0:N], [(ONESB[:], PTh[:, mc, 0:N]) for mc in range(2)], r=[bPTh, bID], w=[bPS[bs]])
                act(LNV[:, 0:N], PSUM[:, bs, 0:N], AF.Ln, r=[bPS[bs]], w=[bLNV])
                act(RSTD[:, 0:N], LNV[:, 0:N], AF.Exp, scale=-1.0, r=[bLNV], w=[bRSTD])
                for j in range(2):
                    c = 2 * h + j
                    b = bank()
                    mm(PSUM[:, b, 0:N], [(VV[:, li, mc, c * 128:(c + 1) * 128], PTh[:, mc, 0:N]) for mc in range(2)],
                       r=[bVV, bPTh], w=[bPS[b]])
                    tt(WA[:, c, 0:N], PSUM[:, b, 0:N], RSTD[:, 0:N], ALU.mult, r=[bPS[b], bRSTD], w=WAw)

            for it in range(5):
                if it < 4:
                    sphase(it)
                if it >= 1:
                    vphase(it - 1)
            proj_residual(tl, Wo, WA, bWA, bRING[io], 8)

        def xattn_sample(tl, li, U, rel):
            N = NS
            iq = U.get(f'wq{li}'); io = U.get(f'wo{li}')
            Wq = wview(iq, 0, [8, 1024]); Wo = wview(io, 0, [8, 1024])
            rmsnorm(tl, 'norm_xa', li)
            for hf in range(2):
                b = bank()
                mm(PSUM[0:NS, b, :], [(tl.HN[:, k, :], Wq[:, k, hf * 512:(hf + 1) * 512]) for k in range(8)],
                   r=[tl.hb, bRING[iq]], w=[bPS[b]])
                cp(QS[:, hf * 512:(hf + 1) * 512], PSUM[0:NS, b, :], r=[bPS[b]], w=[bQS], eng='act')
            bo = bank()
            reserved.add(bo)

            def v2(t):
                return t[:].bitcast(BF16)[:, :, 0:1024] if False else t[:].bitcast(BF16).rearrange("p a b -> p (a b)")[:, 0:2048].rearrange("p (a b) -> p a b", a=2)
            slots = [
                (KS[:], VS[:], bKS, bVS),
                (HNP[:, 0:4, :].rearrange("p (a b) c -> p a (b c)", a=2), HNP[:, 4:8, :].rearrange("p (a b) c -> p a (b c)", a=2), bHNP, bHNP),
                (WB[:, 0:4, :].rearrange("p (a b) c -> p a (b c)", a=2), WB[:, 4:8, :].rearrange("p (a b) c -> p a (b c)", a=2), bWB, bWB),
                (v2(FB[1]), v2(FB[2]), bF[1], bF[2]),
                (v2(FB[3]), v2(FB[4]), bF[3], bF[4]),
            ]
            NSL = len(slots)

            def kv_load(s2):
                kd, vd, kb, vb = slots[s2 % NSL]
                dma('pool', kd, ck[li, s2].rearrange("(mc p) d -> p mc d", p=128), r=tl.xb, w=[kb])
                dma('pool', vd, cv[li, s2].rearrange("(mc p) d -> p mc d", p=128), r=tl.xb, w=[vb])

            for s2 in range(min(NSL - 1, NS)):
                kv_load(s2)
            def qbc(s):
                qb = []
                for hf in range(2):
                    b = bank()
                    mm(PSUM[:, b, :], [(IDB[0:NS, s:s + 1].broadcast_to([NS, 128]), QS[:, hf * 512:(hf + 1) * 512])],
                       r=[bID, bQS], w=[bPS[b]])
                    qb.append(b)
                return qb

            qb_next = qbc(0)
            for s in range(NS):
                KSs, VSs, bKSs, bVSs = slots[s % NSL]
                if s + NSL - 1 < NS:
                    kv_load(s + NSL - 1)
                qb = qb_next
                if s + 1 < NS:
                    qb_next = qbc(s + 1)
                for mc in range(2):
                    for h in range(4):
                        b = qb[h // 2]
                        stt(FB[0][:, 0, 0:256], KSs[:, mc, h * 256:(h + 1) * 256], 1.0,
                            PSUM[:, b, (h % 2) * 256:(h % 2 + 1) * 256], ALU.mult, ALU.mult,
                            r=[bKSs, bPS[b]], w=[bF[0], bSCORE], accum=SCORE[:, mc * 4 + h:mc * 4 + h + 1])
                act(PB[:], SCORE[:], AF.Exp, scale=1.0 / 16.0, r=[bSCORE], w=[bPB])

                def fn(e, s=s, VSs=VSs):
                    ins = None
                    for c in range(8):
                        h = c // 2
                        for mc in range(2):
                            ins = e.matmul(PSUM[:, bo, c * NS + s:c * NS + s + 1], VSs[:, mc, c * 128:(c + 1) * 128],
                                           PB[:, mc * 4 + h:mc * 4 + h + 1], start=(mc == 0), stop=(mc == 1))
                    for h in range(4):
                        for mc in range(2):
                            ins = e.matmul(PSUM[:, bo, 128 + h * NS + s:128 + h * NS + s + 1], ONESB[:],
                                           PB[:, mc * 4 + h:mc * 4 + h + 1], start=(mc == 0), stop=(mc == 1))
                    return ins
                Tk.op('pe', fn, [bVSs, bPB, bID], [bPS[bo]])
            Tk.op('dve', lambda e: e.reciprocal(out=RSS[:].rearrange("p h s -> p (h s)"), in_=PSUM[:, bo, 128:192]),
                  [bPS[bo]], [bRSS])
            for c in range(8):
                tt(WA[:, c, 0:NS], PSUM[:, bo, c * NS:(c + 1) * NS], RSS[:, c // 2, :], ALU.mult,
                   r=[bPS[bo], bRSS], w=WAw)
            reserved.discard(bo)
            proj_residual(tl, Wo, WA, bWA, bRING[io], 8)

        def ffn_sublayer(tiles, li, U):
            for tl in tiles:
                rmsnorm(tl, 'norm_ffn', li)
            GP = FB[0]; bGP = bF[0]
            T1 = FB[1]; T2 = FB[2]; bT1 = bF[1]; bT2 = bF[2]; T3 = FB[3]; bT3 = bF[3]

            def hgu(tl, par):
                if tl.samp:
                    return WBS, 4 * par, bWBS[par]
                return WB, 4 * par, bWBh[par]

            def up(pc):
                iu = U.get(f'up{li}_{pc}')
                Wg = wview(iu, 0, [8, 512]); Wu = wview(iu, 4096, [8, 512])
                for tl in tiles:
                    N = tl.N
                    HG, ko, bHG = hgu(tl, pc % 2)
                    for fl in range(4):
                        f = pc * 4 + fl
                        par = fl % 2
                        bg = bank()
                        bu = bank()

                        def fn2(e, bg=bg, bu=bu, fl=fl, tl=tl, N=N):
                            ins = None
                            for k in range(8):
                                ins = e.matmul(PSUM[:, bg, 0:N], Wg[:, k, fl * 128:(fl + 1) * 128], tl.HN[:, k, 0:N],
                                               start=(k == 0), stop=(k == 7))
                            for k in range(8):
                                ins = e.matmul(PSUM[:, bu, 0:N], Wu[:, k, fl * 128:(fl + 1) * 128], tl.HN[:, k, 0:N],
                                               start=(k == 0), stop=(k == 7))
                            return ins
                        Tk.op('pe', fn2, [tl.hb, bRING[iu]], [bPS[bg], bPS[bu]])
                        if not tl.samp:
                            cp(GP[:, par, 0:2], FHALO[:, li, f, :], r=[bFHALO], w=[bGPp[par]], eng='act')
                            cp(GP[:, par, 2:2 + N], PSUM[:, bg, 0:N], r=[bPS[bg]], w=[bGPp[par]], eng='act')
                            cp(FHALO[:, li, f, :], GP[:, par, N:N + 2], r=[bGPp[par]], w=[bFHALO], eng='act')
                            src = lambda k, par=par, N=N: GP[:, par, k:k + N]
                            srcb = bGPp[par]
                        else:
                            cp(FSS[:, f, :, 2], PSUM[:, bg, 0:N], r=[bPS[bg]], w=[bFSS], eng='act')
                            src = lambda k, f=f: FSS[:, f, :, k]
                            srcb = bFSS
                        cw = lambda k, f=f: P('ffn_conv_w', li * 72 + k * 24 + f)
                        act(T1[:, par, 0:N], PSUM[:, bg, 0:N], AF.Identity, bias=P('ffn_conv_b', li * 24 + f), scale=cw(2),
                            r=[bPS[bg], bPAR], w=[bT1p[par]])
                        stt(PSUM[:, bg, 0:N], src(1), cw(1), T1[:, par, 0:N], ALU.mult, ALU.add,
                            r=[srcb, bPAR, bT1p[par]], w=[bPS[bg]])
                        stt(PSUM[:, bg, 0:N], src(0), cw(0), PSUM[:, bg, 0:N], ALU.mult, ALU.add, r=[srcb, bPAR], w=[bPS[bg]])
                        act(T2[:, par, 0:N], PSUM[:, bg, 0:N], AF.Gelu_apprx_tanh, r=[bPS[bg]], w=[bT2p[par]])
                        tt(HG[:, ko + fl, 0:N], T2[:, par, 0:N], PSUM[:, bu, 0:N], ALU.mult, r=[bT2p[par], bPS[bu]], w=[bHG])
                U.rel(f'up{li}_{pc}')

            def dn(pc):
                idn = U.get(f'dn{li}_{pc}')
                Wd = wview(idn, 0, [4, 1024])
                for tl in tiles:
                    HG, ko, bHG = hgu(tl, pc % 2)
                    proj_residual(tl, Wd, HG, bHG, bRING[idn], 4, koff=ko)
                U.rel(f'dn{li}_{pc}')

            for it in range(7):
                if it < 6:
                    up(it)
                if it >= 1:
                    dn(it - 1)

        def final_out(tl, t0):
            N = tl.N
            stg = lambda c: FB[2 + c // 2][:, c % 2, 0:N]
            act(WB[:, :, 0:N], tl.X[:, :, 0:N], AF.Square, r=tl.xb, w=[bWB])
            b = bank()
            mm(PSUM[:, b, 0:N], [(ONESB[:], WB[:, k, 0:N]) for k in range(8)], r=[bWB, bID], w=[bPS[b]])
            act(LNV[:, 0:N], PSUM[:, b, 0:N], AF.Ln, bias=C_EPS, scale=1.0 / D, r=[bPS[b], bCST], w=[bLNV])
            act(RSTD[:, 0:N], LNV[:, 0:N], AF.Exp, scale=-0.5, r=[bLNV], w=[bRSTD])
            for c in range(8):
                stt(stg(c), tl.X[:, c, 0:N], P('norm_final', c), RSTD[:, 0:N], ALU.mult, ALU.mult,
                    r=[tl.xb[c], bRSTD, bPAR], w=[bF[2 + c // 2]])
            fbufs = [bF[2 + c // 2] for c in range(8)]
            if tl.samp:
                store_T(lambda c: stg(c), NS, ys[:, :], fbufs)
            else:
                for tq in range(4):
                    store_T(lambda c, tq=tq: FB[2 + c // 2][:, c % 2, tq * 128:(tq + 1) * 128], 128,
                            yp[t0 + tq * 128:t0 + (tq + 1) * 128, :], fbufs)

        U = Units(unit_list())

        def main_schedule():
            for mh in range(2):
                load_T(memp[mh * 128:(mh + 1) * 128, :], 128, lambda c, mh=mh: XP[:, c, mh * 128:(mh + 1) * 128], bXP,
                       half=mh)
            ckpt(1)
            for li in range(2):
                mem_kv(li, U)
            ckpt(2)
            load_T(xs[:, :], NS, lambda c: XS[:, c, :], bXS)
            load_T(st_h[:, :], NS, lambda c: H0S[:, c, :], bH0S)
            load_T(st_lc[:, :], NS * 3, lambda c: RECS[:, c, :, 0:3], bRECS,
                   inre=lambda a: a.rearrange("p (b j) -> p b j", j=3))
            for q in range(4):
                load_T(st_cc[q * 120:(q + 1) * 120, :], 120, lambda c, q=q: GBS[:, c, q * 4:(q + 1) * 4, 0:30], bGBS,
                       inre=lambda a: a.rearrange("p (b j) -> p b j", j=30))
            ckpt(3)
            dma('sp', o_lc_s[:, 0:2, :], st_lc.rearrange("(b j) d -> b j d", j=3)[:, 1:3, :])
            dma('sp', o_cc_s[:, 0:29, :], st_cc.rearrange("(b j) d -> b j d", j=30)[:, 1:30, :])
            for li in range(2):
                dma('sp', o_fc_s[li, :, 0:1, :], st_fc[li].rearrange("(b j) d -> b j d", j=2)[:, 1:2, :])
            ckpt(4)

            for g in range(4):
                t0 = g * NT
                U.pfx = str(g)
                cur_g[0] = g
                tiles = [tp] + ([tsm] if g == 0 else [])
                nt = len(tiles)
                for hq in range(4):
                    load_T(xp[t0 + hq * 128:t0 + (hq + 1) * 128, :], 128,
                           lambda c, hq=hq: XP[:, c, hq * 128:(hq + 1) * 128], bXP, half=hq % 2)
                for li in range(2):
                    if li == 0:
                        for ti, tl in enumerate(tiles):
                            lru_sublayer(tl, U, ti == nt - 1)
                            ckpt(10 + g * 100 + li * 10 + ti)
                    else:
                        cmod_sublayer(tiles, U)
                        ckpt(10 + g * 100 + li * 10)
                    for ti, tl in enumerate(tiles):
                        if tl.samp:
                            xattn_sample(tl, li, U, True)
                        else:
                            xattn_prompt(tl, li, U, True)
                        ckpt(12 + g * 100 + li * 10 + ti)
                    U.rel(f'wq{li}'); U.rel(f'wo{li}')
                    if g == 0:
                        for f3 in range(3):
                            load_T(st_fc[li, :, f3 * 1024:(f3 + 1) * 1024], NS * 2,
                                   lambda c, f3=f3: FSS[:, f3 * 8 + c, :, 0:2], bFSS,
                                   inre=lambda a: a.rearrange("p (b j) -> p b j", j=2))
                    ffn_sublayer(tiles, li, U)
                    ckpt(14 + g * 100 + li * 10)
                    if g == 0:
                        for f3 in range(3):
                            store_T(lambda c, f3=f3: FSS[:, f3 * 8 + c, :, 2], NS,
                                    o_fc_s[li, :, 1, f3 * 1024:(f3 + 1) * 1024], bFSS)
                for ti, tl in enumerate(tiles):
                    final_out(tl, t0)
                    ckpt(90 + g * 100 + ti)
                if g == 0:
                    store_T(lambda c: H0S[:, c, :], NS, o_lh_s[:, :], bH0S)
                    ckpt(93)
                    store_T(lambda c: RECS[:, c, :, 3], NS, o_lc_s[:, 2, :], bRECS)
                    ckpt(94)
                    store_T(lambda c: GNS[:, c, :], NS, o_cc_s[:, 29, :], bGNS)
                ckpt(99 + g * 100)
            store_T(lambda c: HST[:, c:c + 1], 1, o_lh_p[:, :], bHST)
            store_T(lambda c: RECH[:, c, :], 3, o_lc_p[:, :], bRECH)
            store_T(lambda c: GST[:, c, :], 30, o_cc_p[:, :], bGST)
            for li in range(2):
                for f3 in range(3):
                    store_T(lambda c, li=li, f3=f3: FHALO[:, li, f3 * 8 + c, :], 2,
                            o_fc_p[li, :, f3 * 1024:(f3 + 1) * 1024], bFHALO)

        try:
            main_schedule()
        except _Stop:
            pass

        Tk.finish()
        with nc.Block() as block:
            Tk.emit(block)
    return nc


_NC_CACHE = {}


def kernel(**inputs):
    f = lambda k: np.asarray(inputs[k], np.float32)
    if 'nc' not in _NC_CACHE:
        _NC_CACHE['nc'] = build_program()
    nc = _NC_CACHE['nc']
    par = pack_params(inputs)
    ident = np.eye(128, dtype=np.float32)
    C = np.ascontiguousarray
    shared = {
        'params': par, 'ident': ident,
        'lru_w_in': C(f('lru_w_in')[0]), 'lru_wa': C(f('lru_wa')[0]), 'lru_wx': C(f('lru_wx')[0]),
        'lru_w_out': C(f('lru_w_out')[0]), 'cm_w_pw1': C(f('cm_w_pw1')[0]), 'cm_w_pw2': C(f('cm_w_pw2')[0]),
        'xa_w_q': C(f('xa_w_q')), 'xa_w_kv': C(f('xa_w_kv')), 'xa_w_o': C(f('xa_w_o')),
        'ffn_w_up': C(f('ffn_w_up')), 'ffn_w_down': C(f('ffn_w_down')),
    }
    in_maps = []
    for b in range(8):
        sl = slice(NS * b, NS * (b + 1))
        m = dict(shared)
        m.update({
            'xp': C(f('x_prompt')[b]), 'xs': C(f('x_sample')[sl, 0, :]),
            'st_h': C(f('state_lru_h')[0, sl]),
            'st_lc': C(f('state_lru_conv')[0, sl].reshape(NS * 3, D)),
            'st_cc': C(f('state_cmod_conv')[0, sl].reshape(NS * 30, D)),
            'st_fc': C(f('state_ffn_conv')[:, sl].reshape(2, NS * 2, DFF)),
            'ck': C(f('cache_mem_k')[:, sl].reshape(2, NS, NMEM, D)),
            'cv': C(f('cache_mem_v')[:, sl].reshape(2, NS, NMEM, D)),
            'memp': C(f('mem_prompt')[b]),
        })
        in_maps.append(m)
    res = run_bass_kernel_spmd(nc, in_maps, core_ids=list(range(8)))
    R = res.results
    B = 8
    y_prompt = np.stack([R[b]['yp'] for b in range(B)]).astype(np.float32)
    y_sample = np.concatenate([R[b]['ys'] for b in range(B)], 0).reshape(128, 1, D).astype(np.float32)
    lh_p = np.stack([R[b]['o_lh_p'].reshape(D) for b in range(B)])[None].astype(np.float32)
    lc_p = np.stack([R[b]['o_lc_p'] for b in range(B)])[None].astype(np.float32)
    cc_p = np.stack([R[b]['o_cc_p'] for b in range(B)])[None].astype(np.float32)
    fc_p = np.stack([R[b]['o_fc_p'] for b in range(B)], 1).astype(np.float32)
    mk = np.stack([R[b]['o_mk'] for b in range(B)], 1).reshape(2, B, NMEM, 4, 256).astype(np.float32)
    mv = np.stack([R[b]['o_mv'] for b in range(B)], 1).reshape(2, B, NMEM, 4, 256).astype(np.float32)
    lh_s = np.concatenate([R[b]['o_lh_s'] for b in range(B)], 0)[None].astype(np.float32)
    lc_s = np.concatenate([R[b]['o_lc_s'] for b in range(B)], 0)[None].astype(np.float32)
    cc_s = np.concatenate([R[b]['o_cc_s'] for b in range(B)], 0)[None].astype(np.float32)
    fc_s = np.concatenate([R[b]['o_fc_s'] for b in range(B)], 1).astype(np.float32)
    return (y_prompt, y_sample, lh_p, lc_p, cc_p, fc_p, mk, mv, lh_s, lc_s, cc_s, fc_s)
```
